# Optimizing a Trainium2 kernel written in Bass

```python
import math
import jax, jax.numpy as jnp
from jax import lax
import numpy as np

D_MODEL = 1024
BATCH = 1
SEQ = 16384
DEPTH = 1
DEC_BATCH = 2
DEC_SEQ = 8192
PAST_LEN = 128

CONV_WIDTH = 512
CONV_KERNEL = 31
N_HEADS = 4
HEAD_DIM = 64
V_DIM = 2 * HEAD_DIM
ATTN_WIDTH = N_HEADS * V_DIM
QK_WIDTH = N_HEADS * 2 * HEAD_DIM
ROPE_THETA = 10000.0
Q_BLOCK = 128
ALPHA = (2.0 * DEPTH) ** 0.25
BETA = (8.0 * DEPTH) ** -0.25
LN_EPS = 1e-5

SPLIT_SIZES = (CONV_WIDTH, CONV_WIDTH, CONV_WIDTH,
               QK_WIDTH, QK_WIDTH, ATTN_WIDTH,
               ATTN_WIDTH,
               D_MODEL, D_MODEL)
IN_WIDTH = sum(SPLIT_SIZES)
SPLIT_POINTS = tuple(int(v) for v in np.cumsum(SPLIT_SIZES)[:-1])

kernel_name = "hybrid_conformer_diffattn_encoder"


def layer_norm(x, g, b):
    xf = x.astype(jnp.float32)
    mu = jnp.mean(xf, axis=-1, keepdims=True)
    var = jnp.mean(jnp.square(xf - mu), axis=-1, keepdims=True)
    y = (xf - mu) * lax.rsqrt(var + LN_EPS) * g.astype(jnp.float32) + b.astype(jnp.float32)
    return y.astype(x.dtype)


def rms_norm(x, g):
    xf = x.astype(jnp.float32)
    y = xf * lax.rsqrt(jnp.mean(jnp.square(xf), axis=-1, keepdims=True) + LN_EPS) * g.astype(jnp.float32)
    return y.astype(x.dtype)


def apply_rope(t):
    S = t.shape[1]
    pos = jnp.arange(S, dtype=jnp.float32)
    inv_freq = 1.0 / jnp.power(ROPE_THETA, jnp.arange(0, HEAD_DIM, 2, dtype=jnp.float32) / HEAD_DIM)
    ang = pos[:, None] * inv_freq[None, :]
    cos = jnp.concatenate([jnp.cos(ang), jnp.cos(ang)], axis=-1)[None, :, None, None, :]
    sin = jnp.concatenate([jnp.sin(ang), jnp.sin(ang)], axis=-1)[None, :, None, None, :]
    tf = t.astype(jnp.float32)
    t1, t2 = jnp.split(tf, 2, axis=-1)
    rot = jnp.concatenate([-t2, t1], axis=-1)
    return (tf * cos + rot * sin).astype(t.dtype)


def diff_attention(q, k, v, lam):
    B, S = q.shape[0], q.shape[1]
    n_blk = S // Q_BLOCK
    qb = q.reshape(B, n_blk, Q_BLOCK, N_HEADS, 2, HEAD_DIM).transpose(1, 0, 2, 3, 4, 5)
    scale = 1.0 / math.sqrt(HEAD_DIM)

    def block(q_blk):
        s = jnp.einsum('bqhcd,bkhcd->bhcqk', q_blk, k).astype(jnp.float32) * scale
        p = jax.nn.softmax(s, axis=-1)
        a = p[:, :, 0] - lam * p[:, :, 1]
        return jnp.einsum('bhqk,bkhe->bqhe', a.astype(v.dtype), v)

    out = lax.map(block, qb)
    return out.transpose(1, 0, 2, 3, 4).reshape(B, S, N_HEADS, V_DIM)


def encoder_layer(x, w_in, b_in, conv_dw, conv_dw_b, conv_ln_g, conv_ln_b, w_conv_proj, b_conv_proj,
                  lam_q1, lam_k1, lam_q2, lam_k2, subln_g, w_attn_o, w_out, b_out, ln_g, ln_b, lam_init):
    B, S, _ = x.shape
    h = x @ w_in + b_in
    glu_a, glu_b, conv_gate, q, k, v, attn_gate, m_conv, m_attn = jnp.split(h, SPLIT_POINTS, axis=-1)

    u = glu_a * jax.nn.sigmoid(glu_b)
    u = lax.conv_general_dilated(u, conv_dw[:, None, :].astype(u.dtype), window_strides=(1,),
                                 padding=[((CONV_KERNEL - 1) // 2, (CONV_KERNEL - 1) // 2)],
                                 dimension_numbers=('NWC', 'WIO', 'NWC'),
                                 feature_group_count=CONV_WIDTH) + conv_dw_b
    u = jax.nn.silu(layer_norm(u, conv_ln_g, conv_ln_b)) * jax.nn.silu(conv_gate)
    y_conv = u @ w_conv_proj + b_conv_proj

    q = apply_rope(q.reshape(B, S, N_HEADS, 2, HEAD_DIM))
    k = apply_rope(k.reshape(B, S, N_HEADS, 2, HEAD_DIM))
    v = v.reshape(B, S, N_HEADS, V_DIM)
    lam = (jnp.exp(jnp.sum(lam_q1.astype(jnp.float32) * lam_k1.astype(jnp.float32)))
           - jnp.exp(jnp.sum(lam_q2.astype(jnp.float32) * lam_k2.astype(jnp.float32))) + lam_init)
    o = diff_attention(q, k, v, lam)
    o = rms_norm(o, subln_g) * (1.0 - lam_init)
    o = o.reshape(B, S, ATTN_WIDTH) * jax.nn.silu(attn_gate)
    y_attn = o @ w_attn_o

    merged = jax.nn.sigmoid(m_conv) * y_conv + jax.nn.sigmoid(m_attn) * y_attn
    out = merged @ w_out + b_out
    return layer_norm(ALPHA * x + out, ln_g, ln_b)


def setup_inputs(seed: int = 0) -> dict:
    key = jax.random.key(seed)
    ks = jax.random.split(key, 24)
    f32 = jnp.float32
    nrm = lambda k, shape, s: jax.random.normal(k, shape, f32) * s
    x_prompt = nrm(ks[0], (BATCH, SEQ, D_MODEL), 1.0)
    x_sample = nrm(ks[1], (DEC_BATCH, DEC_SEQ, D_MODEL), 1.0)
    w_in = nrm(ks[2], (DEPTH, D_MODEL, IN_WIDTH), D_MODEL ** -0.5)
    v_lo = SPLIT_POINTS[4]
    v_hi = SPLIT_POINTS[5]
    col_scale = jnp.ones((IN_WIDTH,), f32).at[v_lo:v_hi].set(BETA)
    w_in = w_in * col_scale
    b_in = nrm(ks[3], (DEPTH, IN_WIDTH), 0.02)
    conv_dw = nrm(ks[4], (DEPTH, CONV_KERNEL, CONV_WIDTH), CONV_KERNEL ** -0.5)
    conv_dw_b = nrm(ks[5], (DEPTH, CONV_WIDTH), 0.02)
    conv_ln_g = 1.0 + nrm(ks[6], (DEPTH, CONV_WIDTH), 0.02)
    conv_ln_b = nrm(ks[7], (DEPTH, CONV_WIDTH), 0.02)
    w_conv_proj = nrm(ks[8], (DEPTH, CONV_WIDTH, D_MODEL), CONV_WIDTH ** -0.5)
    b_conv_proj = nrm(ks[9], (DEPTH, D_MODEL), 0.02)
    lam_q1 = nrm(ks[10], (DEPTH, HEAD_DIM), 0.1)
    lam_k1 = nrm(ks[11], (DEPTH, HEAD_DIM), 0.1)
    lam_q2 = nrm(ks[12], (DEPTH, HEAD_DIM), 0.1)
    lam_k2 = nrm(ks[13], (DEPTH, HEAD_DIM), 0.1)
    subln_g = 1.0 + nrm(ks[14], (DEPTH, V_DIM), 0.02)
    w_attn_o = nrm(ks[15], (DEPTH, ATTN_WIDTH, D_MODEL), ATTN_WIDTH ** -0.5)
    w_out = nrm(ks[16], (DEPTH, D_MODEL, D_MODEL), BETA * D_MODEL ** -0.5)
    b_out = nrm(ks[17], (DEPTH, D_MODEL), 0.02)
    ln_g = 1.0 + nrm(ks[18], (DEPTH, D_MODEL), 0.02)
    ln_b = nrm(ks[19], (DEPTH, D_MODEL), 0.02)
    return {"x_prompt": x_prompt, "x_sample": x_sample, "w_in": w_in, "b_in": b_in,
            "conv_dw": conv_dw, "conv_dw_b": conv_dw_b, "conv_ln_g": conv_ln_g, "conv_ln_b": conv_ln_b,
            "w_conv_proj": w_conv_proj, "b_conv_proj": b_conv_proj,
            "lam_q1": lam_q1, "lam_k1": lam_k1, "lam_q2": lam_q2, "lam_k2": lam_k2,
            "subln_g": subln_g, "w_attn_o": w_attn_o, "w_out": w_out, "b_out": b_out,
            "ln_g": ln_g, "ln_b": ln_b}


def reference(x_prompt, x_sample, w_in, b_in, conv_dw, conv_dw_b, conv_ln_g, conv_ln_b, w_conv_proj,
              b_conv_proj, lam_q1, lam_k1, lam_q2, lam_k2, subln_g, w_attn_o, w_out, b_out, ln_g, ln_b):
    y_prompt = x_prompt
    y_sample = x_sample
    for l in range(DEPTH):
        lam_init = 0.8 - 0.6 * math.exp(-0.3 * l)
        y_prompt = encoder_layer(y_prompt, w_in[l], b_in[l], conv_dw[l], conv_dw_b[l], conv_ln_g[l],
                                 conv_ln_b[l], w_conv_proj[l], b_conv_proj[l], lam_q1[l], lam_k1[l],
                                 lam_q2[l], lam_k2[l], subln_g[l], w_attn_o[l], w_out[l], b_out[l],
                                 ln_g[l], ln_b[l], lam_init)
        y_sample = encoder_layer(y_sample, w_in[l], b_in[l], conv_dw[l], conv_dw_b[l], conv_ln_g[l],
                                 conv_ln_b[l], w_conv_proj[l], b_conv_proj[l], lam_q1[l], lam_k1[l],
                                 lam_q2[l], lam_k2[l], subln_g[l], w_attn_o[l], w_out[l], b_out[l],
                                 ln_g[l], ln_b[l], lam_init)
    return (y_prompt, y_sample)
```

```python
import numpy as np
from contextlib import ExitStack
import concourse.bass as bass
import concourse.mybir as mybir
from concourse.bass_utils import run_bass_kernel_spmd

F32 = mybir.dt.float32
BF16 = mybir.dt.bfloat16
AF = mybir.ActivationFunctionType
ALU = mybir.AluOpType

NCORES = 8
D = 1024
HD = 64
NH = 4
CONVK = 31
LN_EPS = 1e-5
DEPTH = 1
ALPHA = (2.0 * DEPTH) ** 0.25
LAM_INIT = 0.8 - 0.6 * 1.0
R_GA, R_GB, R_CG, R_Q, R_AG, R_MC, R_MA = 0, 512, 1024, 1536, 2048, 2560, 3584
NR = 4608
HALO = 16

ENGS = ("pe", "act", "dve", "pool", "sp")


class Tracker:
    def __init__(self, nc, es):
        self.nc = nc
        self.es = es
        self.prog = {e: [] for e in ENGS}
        self.semobj = {}
        self.semval = {}
        self.waited = {e: {} for e in ENGS}
        self.lastw = {}
        self.readers = {}
        for e in ("pe", "act", "dve", "pool"):
            self.new_sem(e)

    def new_sem(self, name):
        self.semobj[name] = self.es.enter_context(self.nc.semaphore("s_" + name))
        self.semval[name] = 0
        return name

    PSUM_KEYS = frozenset(("S0a", "S0b", "S1a", "S1b", "O0", "O1", "SUM", "AUX"))

    def _excl(self, reads, writes):
        r = tuple(k for k in reads if k not in self.PSUM_KEYS)
        w = tuple(writes) + tuple(k for k in reads if k in self.PSUM_KEYS)
        return r, w

    def _deps(self, reads, writes):
        deps = {}

        def add(tok):
            if tok is not None:
                deps[tok[0]] = max(deps.get(tok[0], 0), tok[1])

        for k in reads:
            add(self.lastw.get(k))
        for k in writes:
            add(self.lastw.get(k))
            for s, v in self.readers.get(k, {}).items():
                add((s, v))
        return deps

    def _waits(self, eng, deps):
        for s, v in deps.items():
            if s == "pe" and eng == "pe":
                continue
            if self.waited[eng].get(s, 0) >= v:
                continue
            self.waited[eng][s] = v
            so = self.semobj[s]
            self.prog[eng].append(lambda e, so=so, v=v: e.wait_ge(so, v))

    def _record(self, tok, reads, writes):
        for k in reads:
            r = self.readers.setdefault(k, {})
            r[tok[0]] = max(r.get(tok[0], 0), tok[1])
        for k in writes:
            self.lastw[k] = tok
            self.readers[k] = {}

    def op(self, eng, fn, reads=(), writes=()):
        reads, writes = self._excl(reads, writes)
        self._waits(eng, self._deps(reads, writes))
        self.semval[eng] += 1
        so = self.semobj[eng]
        self.prog[eng].append(lambda e, fn=fn, so=so: fn(e).then_inc(so, 1))
        tok = (eng, self.semval[eng])
        self._record(tok, reads, writes)
        return tok

    def group(self, eng, fns, reads=(), writes=()):
        reads, writes = self._excl(reads, writes)
        self._waits(eng, self._deps(reads, writes))
        self.semval[eng] += 1
        so = self.semobj[eng]
        for fn in fns[:-1]:
            self.prog[eng].append(lambda e, fn=fn: fn(e))
        self.prog[eng].append(lambda e, fn=fns[-1], so=so: fn(e).then_inc(so, 1))
        tok = (eng, self.semval[eng])
        self._record(tok, reads, writes)
        return tok

    def dma(self, eng, sem, fns, reads=(), writes=()):
        self._waits(eng, self._deps(reads, writes))
        so = self.semobj[sem]
        for fn in fns:
            self.semval[sem] += 16
            self.prog[eng].append(lambda e, fn=fn, so=so: fn(e).then_inc(so, 16))
        tok = (sem, self.semval[sem])
        self._record(tok, reads, writes)
        return tok

    def barrier(self):
        deps = {s: v for s, v in self.semval.items() if v > 0}
        for e in ENGS:
            self._waits(e, dict(deps))

    def replay(self, block):
        prog = self.prog

        @block.tensor
        def _(e):
            for f in prog["pe"]:
                f(e)

        @block.scalar
        def _(e):
            for f in prog["act"]:
                f(e)

        @block.vector
        def _(e):
            for f in prog["dve"]:
                f(e)

        @block.gpsimd
        def _(e):
            for f in prog["pool"]:
                f(e)

        @block.sync
        def _(e):
            for f in prog["sp"]:
                f(e)


def MM(out, lhsT, rhs, start=True, stop=True, **kw):
    return lambda g: g.matmul(out, lhsT=lhsT, rhs=rhs, start=start, stop=stop, **kw)


def DMA(out, in_):
    return lambda g: g.dma_start(out=out, in_=in_)


def ACTF(out, in_, func, **kw):
    return lambda g: g.activation(out=out, in_=in_, func=func, **kw)


def TT(out, in0, in1, op):
    return lambda g: g.tensor_tensor(out=out, in0=in0, in1=in1, op=op)


def STT(out, in0, scalar, in1, op0, op1):
    return lambda g: g.scalar_tensor_tensor(out=out, in0=in0, scalar=scalar, in1=in1, op0=op0, op1=op1)


def TS(out, in0, s1, s2, op0, op1=None):
    if op1 is None:
        return lambda g: g.tensor_scalar(out=out, in0=in0, scalar1=s1, scalar2=None, op0=op0)
    return lambda g: g.tensor_scalar(out=out, in0=in0, scalar1=s1, scalar2=s2, op0=op0, op1=op1)


def CP(out, in_):
    return lambda g: g.tensor_copy(out=out, in_=in_)


def RED(out, in_, op):
    return lambda g: g.tensor_reduce(out=out, in_=in_, axis=mybir.AxisListType.X, op=op)


def MEMSET(ap, v):
    return lambda g: g.memset(ap, v)


def RECIP(out, in_):
    return lambda g: g.reciprocal(out=out, in_=in_)


def build_program(segs, debug=False, stage=99):
    NTOK = sum(s for s, _ in segs)
    NOWN = sum(q for _, q in segs)
    SMAX = max(s for s, _ in segs)
    QB = 512
    XO = sum(q + 2 * HALO for _, q in segs)
    NQB = NOWN // QB

    nc = bass.Bass("TRN2", target_bir_lowering=False)
    es = ExitStack()

    def din(name, shape, dt=F32):
        return nc.dram_tensor(name, list(shape), dt, kind="ExternalInput").ap()

    xT_all = din("xT_all", [D, NTOK])
    xT_own = din("xT_own", [D, XO])
    x_own = din("x_own", [NOWN, D])
    hmask = din("hmask", [128, NQB * 32])
    w_kv = din("w_kv", [D, 1024])
    w_r = din("w_r", [D, NR])
    cvals = din("cvals", [128, 1024])
    cosk = din("cosk", [128, SMAX])
    sink = din("sink", [128, SMAX])
    cosq = din("cosq", [128, NOWN])
    sinq = din("sinq", [128, NOWN])
    w_cp = din("w_cp", [512, D])
    lamv = din("lamv", [1, 256])
    w_ao = din("w_ao", [512, D])
    w_o = din("w_o", [D, D])
    rowv = din("rowv", [3, D])
    y_own = nc.dram_tensor("y_own", [NOWN, D], F32, kind="ExternalOutput").ap()
    skind = "ExternalOutput" if debug else "Internal"
    kT_scr = nc.dram_tensor("kT_scr", [NH, 128, NTOK], BF16, kind=skind).ap()
    v_scr = nc.dram_tensor("v_scr", [NH, 128, NTOK // 128, 128], BF16, kind=skind).ap()

    def sb(name, shape, dt):
        return es.enter_context(nc.sbuf_tensor(name, list(shape), dt))

    Wkv = sb("Wkv", [128, 8, 1024], BF16)
    Wr = sb("Wr", [128, 8, NR], BF16)
    Wcp = sb("Wcp", [128, 4, D], BF16)
    Wao = sb("Wao", [128, 4, D], BF16)
    Wo = sb("Wo", [128, 8, D], BF16)
    NF = 13
    ft = [sb(f"f{i}", [128, 512], F32) for i in range(NF)]
    xb = [sb(f"xb{i}", [128, 8, 512], BF16) for i in range(2)]
    kbf = [sb(f"kbf{i}", [128, 512], BF16) for i in range(2)]
    kTo = [sb(f"kTo{i}", [128, 4, 512], BF16) for i in range(2)]
    vo = [sb(f"vo{i}", [128, 2048], BF16) for i in range(2)]
    uin = sb("uin", [128, 4, 512 + 2 * HALO], F32)
    bro = [sb(f"bro{i}", [128, D], F32) for i in range(3)]
    xhb = sb("xhb", [128, 8, 2 * HALO], BF16)
    csb = sb("csb", [128, 1024], F32)
    dyn = sb("dyn", [128, 16], F32)
    cbf = sb("cbf", [128, 264], BF16)
    hm = sb("hm", [128, 32], F32)
    lamt = sb("lamt", [1, 256], F32)
    small = sb("small", [128, 8], F32)
    B_R, B_K, B_V, B_CP, C_SUBG = 0, 36, 40, 44, 52
    C_SEL, C_CW, C_CV, C_ONES, C_PERM, C_BSEL = 56, 64, 188, 200, 328, 456
    Y_NEGLAM, Y_OML, Y_GS, Y_CB, Y_EPS = 0, 1, 2, 4, 8
    perm_bf = cbf[:, 0:128]
    selA_bf, selB_bf = cbf[:, 256:258], cbf[:, 258:260]
    selA_f, selB_f = csb[:, C_SEL:C_SEL + 2], csb[:, C_SEL + 2:C_SEL + 4]
    ones_row_f = csb[0:1, C_ONES:C_ONES + 128]
    ones_col_f = csb[:, C_ONES:C_ONES + 1]
    indA_f = csb[:, C_BSEL:C_BSEL + 128]
    indB_f = csb[:, C_BSEL + 128:C_BSEL + 256]
    C_Z = 712
    zrow0_f = csb[:, C_Z + 64:C_Z + 192]
    zrow64_f = csb[:, C_Z:C_Z + 128]
    ones1_bf = cbf[:, 128:129]
    pS = [es.enter_context(nc.psum_tensor(f"pS{i}", [128, 1024], F32)) for i in range(2)]
    pOt = [es.enter_context(nc.psum_tensor(f"pO{i}", [128, 512], F32)) for i in range(2)]
    pSum = es.enter_context(nc.psum_tensor("pSum", [128, 512], F32))
    pAux = es.enter_context(nc.psum_tensor("pAux", [128, 512], F32))
    banks = {"S0a": pS[0][:, 0:512], "S0b": pS[0][:, 512:1024], "S1a": pS[1][:, 0:512],
             "S1b": pS[1][:, 512:1024], "O0": pOt[0][:, :], "O1": pOt[1][:, :], "SUM": pSum[:, :],
             "AUX": pAux[:, :]}

    T = Tracker(nc, es)
    for i in range(NF):
        T.new_sem(f"df{i}")
    for i in range(8):
        T.new_sem(f"dkv{i}")
    for n in ("dkTo0", "dkTo1", "dvo0", "dvo1", "dmisc", "dout0", "dout1", "dhm"):
        T.new_sem(n)

    free_f = list(range(NF))
    free_x = []

    def falloc():
        return free_f.pop(0)

    def falloc_x():
        return free_x.pop(0)

    def ffree(*idx):
        for i in idx:
            free_f.append(i)

    rr = {"cast": 0}

    def cast_op(out_ap, in_ap, reads, writes, engs=("dve", "pool", "act")):
        e = engs[rr["cast"] % len(engs)]
        rr["cast"] += 1
        if e == "act":
            return T.op("act", ACTF(out_ap, in_ap, AF.Copy), reads, writes)
        return T.op(e, CP(out_ap, in_ap), reads, writes)

    def finish():
        T.barrier()
        with nc.Block() as block:
            T.replay(block)
        es.close()
        return nc

    T.dma("sp", "dmisc", [
        DMA(csb[:, :], cvals[:, :]),
        DMA(lamt[:, :], lamv[:, :]),
        DMA(bro[0][:, :], rowv[0:1, :].partition_broadcast(128)),
        DMA(bro[1][:, :], rowv[1:2, :].partition_broadcast(128)),
        DMA(bro[2][:, :], rowv[2:3, :].partition_broadcast(128)),
    ], writes=("csb", "lamt", "bro"))
    for _b in banks:
        T.op("dve", MEMSET(banks[_b], 0.0), (), (_b,))
    if stage == 0:
        return finish()
    T.op("dve", MEMSET(dyn[:, Y_EPS:Y_EPS + 1], LN_EPS), (), ("dyn",))
    T.op("dve", CP(cbf[:, 0:128], csb[:, C_PERM:C_PERM + 128]), ("csb",), ("cbf",))
    T.op("dve", CP(cbf[:, 128:256], csb[:, C_ONES:C_ONES + 128]), ("csb",), ("cbf",))
    T.op("dve", CP(cbf[:, 256:260], csb[:, C_SEL:C_SEL + 4]), ("csb",), ("cbf",))
    T.op("dve", TT(lamt[:, 0:64], lamt[:, 0:64], lamt[:, 64:128], ALU.mult), ("lamt",), ("lamt",))
    T.op("dve", TT(lamt[:, 128:192], lamt[:, 128:192], lamt[:, 192:256], ALU.mult), ("lamt",), ("lamt",))
    T.op("dve", RED(lamt[:, 64:65], lamt[:, 0:64], ALU.add), ("lamt",), ("lamt",))
    T.op("dve", RED(lamt[:, 65:66], lamt[:, 128:192], ALU.add), ("lamt",), ("lamt",))
    T.op("act", ACTF(lamt[:, 66:68], lamt[:, 64:66], AF.Exp), ("lamt",), ("lamt",))
    T.op("dve", TT(lamt[:, 68:69], lamt[:, 66:67], lamt[:, 67:68], ALU.subtract), ("lamt",), ("lamt",))
    T.op("dve", TS(lamt[:, 70:71], lamt[:, 68:69], -1.0, -LAM_INIT, ALU.mult, ALU.add), ("lamt",), ("lamt",))
    T.op("dve", TS(lamt[:, 71:72], lamt[:, 68:69], -1.0, 1.0 - LAM_INIT, ALU.mult, ALU.add), ("lamt",), ("lamt",))
    T.group("pe", [MM(banks["AUX"][:, 0:2], ones_row_f, lamt[0:1, 70:72])], ("csb", "lamt"), ("AUX",))
    T.op("dve", CP(dyn[:, Y_NEGLAM:Y_NEGLAM + 2], banks["AUX"][:, 0:2]), ("AUX",), ("dyn",))
    T.op("dve", TS(dyn[:, Y_GS:Y_GS + 1], csb[:, C_SUBG:C_SUBG + 1], 1.0 - LAM_INIT, None, ALU.mult),
         ("csb", "dyn"), ("dyn",))
    T.op("dve", TS(dyn[:, Y_CB:Y_CB + 4], csb[:, B_V:B_V + 4], dyn[:, Y_OML:Y_OML + 1], None, ALU.mult),
         ("csb", "dyn"), ("dyn",))

    if stage == 1:
        return finish()

    wq = []

    def load_weight(dst, src, nrow_chunks, ncols, key, defer=False):
        for c in range(nrow_chunks):
            for g0 in range(0, ncols, 512):
                item = (dst[:, c, g0:g0 + 512], src[c * 128:(c + 1) * 128, g0:g0 + 512], key)
                if defer:
                    wq.append(item)
                else:
                    emit_wload(item, ("dve", "pool", "act"))

    def emit_wload(item, engs, ring=False):
        dst_ap, src_ap, key = item
        fi = falloc_x() if ring else falloc()
        T.dma("sp", f"df{fi}", [DMA(ft[fi][:, :], src_ap)], writes=(f"f{fi}",))
        cast_op(dst_ap, ft[fi][:, :], (f"f{fi}",), (key,), engs=engs)
        if ring:
            free_x.append(fi)
        else:
            ffree(fi)

    load_weight(Wkv, w_kv, 8, 1024, "Wkv")
    load_weight(Wr, w_r, 8, NR, "Wr", defer=True)
    load_weight(Wcp, w_cp, 4, D, "Wcp", defer=True)
    load_weight(Wao, w_ao, 4, D, "Wao", defer=True)
    load_weight(Wo, w_o, 8, D, "Wo", defer=True)

    if stage == 2:
        return finish()

    def rope_chunk(src_key, src_ps, bias_col, cos_ap, sin_ap, cos_key, sin_key, out_ap, out_key, rot_bank, kb_i,
                   final_eng="pool"):
        kb = kbf[kb_i]
        kbk = f"kbf{kb_i}"
        T.op("act", ACTF(kb[:, :], src_ps, AF.Identity, bias=bias_col, scale=1.0), (src_key, "csb"), (kbk,))
        T.group("pe", [MM(banks[rot_bank], perm_bf, kb[:, :])], (kbk, "cbf"), (rot_bank,))
        a = falloc()
        b = falloc()
        T.op("dve", STT(ft[a][:, :], src_ps, bias_col, cos_ap, ALU.add, ALU.mult), (src_key, "csb", cos_key),
             (f"f{a}",))
        T.op("dve", TT(ft[b][:, :], banks[rot_bank], sin_ap, ALU.mult), (rot_bank, sin_key), (f"f{b}",))
        T.op(final_eng, TT(out_ap, ft[a][:, :], ft[b][:, :], ALU.add), (f"f{a}", f"f{b}"), (out_key,))
        ffree(a, b)

    seg_tok0 = []
    t0 = 0
    for S, nq in segs:
        seg_tok0.append(t0)
        t0 += S
    pend_st = []
    for _ in range(8):
        free_x.append(free_f.pop())
    for t in range(NTOK // 512):
        tok0 = t * 512
        si = max(i for i in range(len(segs)) if seg_tok0[i] <= tok0)
        pos0 = tok0 - seg_tok0[si]
        xbi = t % 2
        xbk = tuple(f"xb{xbi}_{c}" for c in range(8))
        for c in range(8):
            fi = falloc_x()
            T.dma("sp", f"df{fi}", [DMA(ft[fi][:, :], xT_all[c * 128:(c + 1) * 128, tok0:tok0 + 512])],
                  writes=(f"f{fi}",))
            cast_op(xb[xbi][:, c, :], ft[fi][:, :], (f"f{fi}",), (xbk[c],), engs=("act",))
            free_x.append(fi)
        for _ in range(2):
            if wq:
                emit_wload(wq.pop(0), ("dve",), ring=True)
        ci = falloc()
        sii = falloc()
        T.dma("sp", f"df{ci}", [DMA(ft[ci][:, :], cosk[:, pos0:pos0 + 512])], writes=(f"f{ci}",))
        T.dma("sp", f"df{sii}", [DMA(ft[sii][:, :], sink[:, pos0:pos0 + 512])], writes=(f"f{sii}",))
        while pend_st:
            pend_st.pop(0)()
        ko = t % 2
        KB = ("S0a", "S0b", "S1a", "S1b")
        for h in range(4):
            T.group("pe", [MM(banks[KB[h]], Wkv[:, c, h * 128:(h + 1) * 128], xb[xbi][:, c, :],
                              start=(c == 0), stop=(c == 7)) for c in range(8)], ("Wkv",) + xbk, (KB[h],))
        vov = vo[ko][:, :].rearrange("p (h j e) -> p h j e", h=4, j=4)
        for i in range(4):
            rope_chunk(KB[i], banks[KB[i]], csb[:, B_K + i:B_K + i + 1], ft[ci][:, :], ft[sii][:, :],
                       f"f{ci}", f"f{sii}", kTo[ko][:, i, :], f"kTo{ko}", ("SUM", "AUX")[i % 2], i % 2,
                       final_eng="dve")
            vbank = ("O0", "O1")[i % 2]
            T.group("pe", [MM(banks[vbank], xb[xbi][:, c, i * 128:(i + 1) * 128], Wkv[:, c, 512:1024],
                              start=(c == 0), stop=(c == 7)) for c in range(8)], ("Wkv",) + xbk, (vbank,))
            T.op("dve", CP(vov[:, :, i, :], banks[vbank].rearrange("p (h e) -> p h e", h=4)),
                 (vbank,), (f"vo{ko}",))
        ffree(ci, sii)
        J0 = tok0 // 128

        def stores(ko=ko, tok0=tok0, J0=J0):
            T.dma("sp", f"dkTo{ko}", [DMA(kT_scr[:, :, tok0:tok0 + 512].rearrange("h p t -> p h t"),
                                          kTo[ko][:, :, :])], reads=(f"kTo{ko}",), writes=("scr",))
            T.dma("sp", f"dvo{ko}", [DMA(v_scr[:, :, J0:J0 + 4, :].rearrange("h p j e -> p h (j e)"),
                                         vo[ko][:, :].rearrange("p (h je) -> p h je", h=4))],
                  reads=(f"vo{ko}",), writes=("scr",))
        pend_st.append(stores)
    while pend_st:
        pend_st.pop(0)()
    while free_x:
        free_f.append(free_x.pop(0))
    while wq:
        emit_wload(wq.pop(0), ("dve", "pool", "act"))

    if stage == 3:
        return finish()
    T.barrier()

    QT, U2 = kTo[0], kTo[1]
    xb1f = xb[1][:, :, :].rearrange("p a b -> p (a b)").bitcast(F32)
    zbuf = [xb1f[:, 0:1024], xb1f[:, 1024:2048]]
    zalias = [tuple(f"ags{h}" for h in range(4)), tuple(f"og{h}" for h in range(4))]
    pT = [vo[0][:, 0:1024], vo[0][:, 1024:2048], vo[1][:, 0:1024], vo[1][:, 1024:2048]]
    pT_key = ["vo0a", "vo0b", "vo1a", "vo1b"]
    merged = [vo[k // 4][:, (k % 4) * 512:(k % 4 + 1) * 512] for k in range(8)]
    merged_key = [pT_key[k // 2] for k in range(8)]

    XALL = tuple(f"x0_{c}" for c in range(8))

    def proj_group(bank, col0, rhs_fn, rkey, n=8, width=None):
        rkeys = XALL if rkey == "xall" else (rkey,)
        out = banks[bank] if width is None else banks[bank][:, 0:width]
        return T.group("pe", [MM(out, Wr[:, c, col0:col0 + 128], rhs_fn(c), start=(c == 0), stop=(c == n - 1))
                              for c in range(n)], ("Wr",) + rkeys, (bank,))

    items = []
    qb_list = []
    own0 = 0
    xo0 = 0
    for si, (S, nq) in enumerate(segs):
        for qb in range(nq // QB):
            qb_list.append((si, S, own0 + qb * QB, xo0 + qb * QB))
            for h in range(4):
                for sbk in range(S // 512):
                    items.append((si, h, sbk))
        own0 += nq
        xo0 += nq + 2 * HALO
    PF = 6
    state = {"next_dma": 0, "step": 0}

    def issue_kv(n):
        si, h, sbk = items[n]
        slot = n % 8
        tok0 = seg_tok0[si] + sbk * 512
        J0 = tok0 // 128
        T.dma("sp", f"dkv{slot}", [
            DMA(Wkv[:, slot, 0:512], kT_scr[h, :, tok0:tok0 + 512]),
            DMA(Wkv[:, slot, 512:1024].rearrange("p (j e) -> p j e", j=4), v_scr[h, :, J0:J0 + 4, :]),
        ], reads=("scr",), writes=(f"kv{slot}",))

    def finalize_stages(h):
        o1, o2, stl, r1, r2 = falloc(), falloc(), falloc(), falloc(), falloc()

        def s0():
            T.op("dve", CP(ft[o1][:, :], banks["O0"]), ("O0",), (f"f{o1}",))
            T.op("dve", CP(ft[o2][:, :], banks["O1"]), ("O1",), (f"f{o2}",))
            T.op("dve", CP(ft[stl][:, :], banks["SUM"]), ("SUM",), (f"f{stl}",))

        def s1():
            T.group("pe", [MM(banks["AUX"], indA_f, ft[stl][:, :])], ("csb", f"f{stl}"), ("AUX",))
            T.op("dve", RECIP(ft[r1][:, :], banks["AUX"]), ("AUX",), (f"f{r1}",))

        def s2():
            T.group("pe", [MM(banks["AUX"], indB_f, ft[stl][:, :])], ("csb", f"f{stl}"), ("AUX",))
            T.op("dve", RECIP(ft[r2][:, :], banks["AUX"]), ("AUX",), (f"f{r2}",))
            ffree(stl)
            T.op("dve", TT(ft[o1][:, :], ft[o1][:, :], ft[r1][:, :], ALU.mult), (f"f{o1}", f"f{r1}"), (f"f{o1}",))
            T.op("dve", TT(ft[o2][:, :], ft[o2][:, :], ft[r2][:, :], ALU.mult), (f"f{o2}", f"f{r2}"), (f"f{o2}",))
            T.op("dve", STT(ft[o1][:, :], ft[o2][:, :], dyn[:, Y_NEGLAM:Y_NEGLAM + 1], ft[o1][:, :], ALU.mult,
                            ALU.add), (f"f{o1}", f"f{o2}", "dyn"), (f"f{o1}",))
            T.op("dve", TS(ft[o1][:, :], ft[o1][:, :], dyn[:, Y_CB + h:Y_CB + h + 1], None, ALU.add),
                 (f"f{o1}", "dyn"), (f"f{o1}",))

        def s3():
            T.op("dve", TT(ft[o2][:, :], ft[o1][:, :], ft[o1][:, :], ALU.mult), (f"f{o1}",), (f"f{o2}",))
            ffree(r2)

        def s4():
            T.group("pe", [MM(banks["AUX"][0:1, :], ones_col_f, ft[o2][:, :])], ("csb", f"f{o2}"), ("AUX",))
            T.op("dve", CP(ft[r1][0:1, :], banks["AUX"][0:1, :]), ("AUX",), (f"f{r1}",))

        def s5():
            T.group("pe", [MM(banks["AUX"], ones_row_f, ft[r1][0:1, :])], ("csb", f"f{r1}"), ("AUX",))
            ffree(r1)

        def s6():
            T.op("act", ACTF(ft[o2][:, :], banks["AUX"], AF.Ln, bias=dyn[:, Y_EPS:Y_EPS + 1], scale=1.0 / 128.0),
                 ("AUX", "dyn"), (f"f{o2}",))
            T.op("act", ACTF(ft[o2][:, :], ft[o2][:, :], AF.Exp, scale=-0.5), (f"f{o2}",), (f"f{o2}",))
            T.op("dve", STT(ft[o1][:, :], ft[o1][:, :], dyn[:, Y_GS:Y_GS + 1], ft[o2][:, :], ALU.mult, ALU.mult),
                 (f"f{o1}", f"f{o2}", "dyn"), (f"f{o1}",))
            T.op("dve", TT(xb[1][:, 4 + h, :], ft[o1][:, :], xb[1][:, h, :], ALU.mult),
                 (f"f{o1}", f"ags{h}"), (f"og{h}", "z1"))
            ffree(o1, o2)

        return s0, [s1, s2, s3, s4, s5, s6]

    def run_gen(g):
        for _ in g:
            pass

    def emit_xload(qbi_, xc0_):
        for c in range(8):
            fi = falloc()
            T.dma("sp", f"df{fi}", [DMA(ft[fi][:, :], xT_own[c * 128:(c + 1) * 128, xc0_ + HALO:xc0_ + HALO + 512])],
                  writes=(f"f{fi}",))
            cast_op(xb[0][:, c, :], ft[fi][:, :], (f"f{fi}",), (XALL[c],), engs=("dve", "pool"))
            ffree(fi)
        fi = falloc()
        hv = ft[fi][:, 0:256].rearrange("p (c s w) -> p c s w", c=8, s=2)
        T.dma("sp", f"df{fi}", [
            DMA(hv[:, :, 0, :], xT_own[:, xc0_:xc0_ + HALO].rearrange("(c p) w -> p c w", p=128)),
            DMA(hv[:, :, 1, :], xT_own[:, xc0_ + HALO + 512:xc0_ + 2 * HALO + 512].rearrange(
                "(c p) w -> p c w", p=128)),
        ], writes=(f"f{fi}",))
        T.op("dve", CP(xhb[:, :, :].rearrange("p c w -> p (c w)"), ft[fi][:, 0:256]), (f"f{fi}",), ("xhb",))
        ffree(fi)
        T.dma("sp", "dhm", [DMA(hm[:, :], hmask[:, qbi_ * 32:(qbi_ + 1) * 32])], writes=("hm",))


    def emit_glu():
        for j in range(4):
            ba, bb = ("O0", "O1", "SUM")[(2 * j) % 3], ("O0", "O1", "SUM")[(2 * j + 1) % 3]
            proj_group(ba, R_GA + j * 128, lambda c: xb[0][:, c, :], "xall")
            proj_group(bb, R_GB + j * 128, lambda c: xb[0][:, c, :], "xall")
            sg = falloc()
            bga = csb[:, B_R + (R_GA // 128) + j:B_R + (R_GA // 128) + j + 1]
            bgb = csb[:, B_R + (R_GB // 128) + j:B_R + (R_GB // 128) + j + 1]
            T.op("act", ACTF(ft[sg][:, :], banks[bb], AF.Sigmoid, bias=bgb, scale=1.0), (bb, "csb"), (f"f{sg}",))
            T.op("dve", STT(uin[:, j, HALO:HALO + 512], banks[ba], bga, ft[sg][:, :], ALU.add, ALU.mult),
                 (ba, "csb", f"f{sg}"), (f"uin{j}",))
            ha, hb = (("S0a", "S0b"), ("S1a", "S1b"))[j % 2]
            proj_group(ha, R_GA + j * 128, lambda c: xhb[:, c, :], "xhb", width=32)
            T.op("dve", TS(ft[sg][:, 32:64], banks[ha][:, 0:32], bga, None, ALU.add), (ha, "csb"),
                 (f"f{sg}",))
            proj_group(hb, R_GB + j * 128, lambda c: xhb[:, c, :], "xhb", width=32)
            T.op("act", ACTF(ft[sg][:, 0:32], banks[hb][:, 0:32], AF.Sigmoid, bias=bgb, scale=1.0),
                 (hb, "csb"), (f"f{sg}",))
            T.op("dve", TT(ft[sg][:, 0:32], ft[sg][:, 0:32], ft[sg][:, 32:64], ALU.mult), (f"f{sg}",), (f"f{sg}",))
            T.op("dve", TT(uin[:, j, 0:HALO], ft[sg][:, 0:HALO], hm[:, 0:HALO], ALU.mult), (f"f{sg}", "hm"),
                 (f"uin{j}",))
            T.op("dve", TT(uin[:, j, HALO + 512:2 * HALO + 512], ft[sg][:, HALO:2 * HALO], hm[:, HALO:2 * HALO],
                           ALU.mult), (f"f{sg}", "hm"), (f"uin{j}",))
            ffree(sg)


    for qbi, (si, S, o0, xc0) in enumerate(qb_list):
        xk = tuple(f"x0_{c}" for c in range(8))
        if qbi == 0:
            emit_xload(qbi, xc0)
            emit_glu()

        cq, sq_ = falloc(), falloc()
        T.dma("sp", f"df{cq}", [DMA(ft[cq][:, :], cosq[:, o0:o0 + 512])], writes=(f"f{cq}",))
        T.dma("sp", f"df{sq_}", [DMA(ft[sq_][:, :], sinq[:, o0:o0 + 512])], writes=(f"f{sq_}",))
        for h in range(4):
            qbank = ("S0a", "S0b")[h % 2]
            rbank = ("S1a", "S1b")[h % 2]
            proj_group(qbank, R_Q + h * 128, lambda c: xb[0][:, c, :], "xall")
            rope_chunk(qbank, banks[qbank], csb[:, B_R + R_Q // 128 + h:B_R + R_Q // 128 + h + 1], ft[cq][:, :],
                       ft[sq_][:, :], f"f{cq}", f"f{sq_}", QT[:, h, :], f"qt{h}", rbank, h % 2)
        ffree(cq, sq_)
        for h in range(4):
            gbank = ("O0", "O1")[h % 2]
            proj_group(gbank, R_AG + h * 128, lambda c: xb[0][:, c, :], "xall")
            T.op("act", ACTF(xb[1][:, h, :], banks[gbank], AF.Silu,
                             bias=csb[:, B_R + R_AG // 128 + h:B_R + R_AG // 128 + h + 1], scale=1.0),
                 (gbank, "csb"), (f"ags{h}", "z0"))

        def conv_stages():
            acc = [falloc() for _ in range(4)]
            sqa, sqb, rstd_t = falloc(), falloc(), falloc()
            mean_t = sqb

            def c0():
                for j in range(4):
                    wcol = lambda k, j=j: csb[:, C_CW + j * CONVK + k:C_CW + j * CONVK + k + 1]
                    T.op("dve", TS(ft[acc[j]][:, :], uin[:, j, 1:513], wcol(0), csb[:, C_CV + j:C_CV + j + 1],
                                   ALU.mult, ALU.add), (f"uin{j}", "csb"), (f"f{acc[j]}",))
                    for k in range(1, CONVK):
                        T.op("dve", STT(ft[acc[j]][:, :], uin[:, j, 1 + k:513 + k], wcol(k), ft[acc[j]][:, :],
                                        ALU.mult, ALU.add), (f"uin{j}", "csb", f"f{acc[j]}"), (f"f{acc[j]}",))

            def sq_pair(j0):
                for j, s in ((j0, sqa), (j0 + 1, sqb)):
                    T.op("pool", TT(ft[s][:, :], ft[acc[j]][:, :], ft[acc[j]][:, :], ALU.mult), (f"f{acc[j]}",),
                         (f"f{s}",))

            def stat_mm(j0):
                T.group("pe", [MM(banks["AUX"], zrow0_f, ft[acc[j0]][:, :], start=(j0 == 0), stop=False),
                               MM(banks["AUX"], zrow64_f, ft[sqa][:, :], start=False, stop=False),
                               MM(banks["AUX"], zrow0_f, ft[acc[j0 + 1]][:, :], start=False, stop=False),
                               MM(banks["AUX"], zrow64_f, ft[sqb][:, :], start=False, stop=(j0 == 2))],
                        ("csb", f"f{acc[j0]}", f"f{acc[j0 + 1]}", f"f{sqa}", f"f{sqb}"), ("AUX",))

            def c1():
                sq_pair(0)

            def c2():
                stat_mm(0)
                sq_pair(2)

            def c3():
                stat_mm(2)
                T.op("dve", CP(ft[sqa][:, :], banks["AUX"]), ("AUX",), (f"f{sqa}",))

            def c4():
                T.group("pe", [MM(banks["AUX"], indA_f, ft[sqa][:, :])], ("csb", f"f{sqa}"), ("AUX",))
                T.op("dve", TS(ft[mean_t][:, :], banks["AUX"], 1.0 / 512.0, None, ALU.mult), ("AUX",),
                     (f"f{mean_t}",))

            def c5():
                T.group("pe", [MM(banks["AUX"], indB_f, ft[sqa][:, :])], ("csb", f"f{sqa}"), ("AUX",))
                T.op("pool", TT(ft[sqa][:, :], ft[mean_t][:, :], ft[mean_t][:, :], ALU.mult), (f"f{mean_t}",),
                     (f"f{sqa}",))
                T.op("dve", STT(ft[rstd_t][:, :], banks["AUX"], 1.0 / 512.0, ft[sqa][:, :], ALU.mult, ALU.subtract),
                     ("AUX", f"f{sqa}"), (f"f{rstd_t}",))

            def c6():
                T.op("act", ACTF(ft[rstd_t][:, :], ft[rstd_t][:, :], AF.Ln, bias=dyn[:, Y_EPS:Y_EPS + 1], scale=1.0),
                     (f"f{rstd_t}", "dyn"), (f"f{rstd_t}",))
                T.op("act", ACTF(ft[rstd_t][:, :], ft[rstd_t][:, :], AF.Exp, scale=-0.5), (f"f{rstd_t}",),
                     (f"f{rstd_t}",))
                T.op("dve", STT(ft[mean_t][:, :], ft[mean_t][:, :], -1.0, ft[rstd_t][:, :], ALU.mult, ALU.mult),
                     (f"f{mean_t}", f"f{rstd_t}"), (f"f{mean_t}",))
                for j in range(4):
                    a = acc[j]
                    T.op("pool", TT(ft[a][:, :], ft[a][:, :], ft[rstd_t][:, :], ALU.mult), (f"f{a}", f"f{rstd_t}"),
                         (f"f{a}",))
                    T.op("pool", TT(ft[a][:, :], ft[a][:, :], ft[mean_t][:, :], ALU.add), (f"f{a}", f"f{mean_t}"),
                         (f"f{a}",))

            def c7():
                for j in range(4):
                    a = acc[j]
                    T.op("act", ACTF(ft[a][:, :], ft[a][:, :], AF.Silu, bias=csb[:, C_CV + 8 + j:C_CV + 9 + j],
                                     scale=csb[:, C_CV + 4 + j:C_CV + 5 + j]), (f"f{a}", "csb"), (f"f{a}",))

            def cg_a(j):
                def f():
                    proj_group("AUX", R_CG + j * 128, lambda c: xb[0][:, c, :], "xall")
                return f

            def cg_b(j):
                def f():
                    a = acc[j]
                    bcg = csb[:, B_R + (R_CG // 128) + j:B_R + (R_CG // 128) + j + 1]
                    T.op("act", ACTF(ft[sqa][:, :], banks["AUX"], AF.Silu, bias=bcg, scale=1.0), ("AUX", "csb"),
                         (f"f{sqa}",))
                    T.op("pool", TT(U2[:, j, :], ft[a][:, :], ft[sqa][:, :], ALU.mult), (f"f{a}", f"f{sqa}"),
                         (f"u2_{j}",))
                    if j == 3:
                        ffree(sqa, sqb, rstd_t, *acc)
                return f

            st = [c1, c2, c3, c4, c5, c6, c7]
            for j in range(4):
                st += [cg_a(j), cg_b(j)]
            holds = {c2}
            for f in st:
                if f.__name__ == "f" and st.index(f) >= 7 and (st.index(f) - 7) % 2 == 0:
                    holds.add(f)
            return c0, [(f, f in holds) for f in st]

        c0, cstages = conv_stages()
        c0()
        cq = list(cstages)
        fq = []
        CONV_MIN_STEP = 100
        aux_live = [False]

        def pop_conv():
            f, holds = cq.pop(0)
            f()
            aux_live[0] = holds

        nsb = S // 512
        nsteps = nsb * 4
        item_base = state.setdefault("item", 0)
        steps = [(h, stp) for h in range(4) for stp in range(nsteps)]

        def ensure_dma(upto):
            while state["next_dma"] <= min(upto, len(items) - 1):
                issue_kv(state["next_dma"])
                state["next_dma"] += 1

        def qk(g):
            h, stp = steps[g]
            n = item_base + h * nsb + stp // 4
            ensure_dma(n + PF)
            slot = n % 8
            kc = stp % 4
            b = g % 2
            T.group("pe", [
                MM(pS[b][:, 0:512], Wkv[0:64, slot, kc * 128:(kc + 1) * 128], QT[0:64, h, :]),
                MM(pS[b][:, 512:1024], Wkv[64:128, slot, kc * 128:(kc + 1) * 128], QT[64:128, h, :]),
            ], (f"kv{slot}", f"qt{h}"), (f"S{b}a", f"S{b}b"))
            return slot, kc

        info = {}
        info[0] = qk(0)
        if len(steps) > 1:
            info[1] = qk(1)
        last_pop = -10
        for g, (h, stp) in enumerate(steps):
            b = g % 2
            pb = g % 4
            slot, kc = info.pop(g)
            T.op("act", ACTF(pT[pb], pS[b][:, 0:1024], AF.Exp, scale=0.125), (f"S{b}a", f"S{b}b"), (pT_key[pb],))
            if g + 2 < len(steps):
                info[g + 2] = qk(g + 2)
            vch = Wkv[:, slot, 512 + kc * 128:512 + (kc + 1) * 128]
            first, last = (stp == 0), (stp == nsteps - 1)
            mms = [MM(banks["O0"], vch, pT[pb][:, 0:512], start=first, stop=last),
                   MM(banks["O1"], vch, pT[pb][:, 512:1024], start=first, stop=last)]
            rk = (f"kv{slot}", pT_key[pb], "cbf")
            wk = ("O0", "O1")
            if stp % 2 == 1:
                pp = (g - 1) % 4
                f2, l2 = (stp == 1), last
                mms += [
                    MM(banks["SUM"][0:1, :], ones1_bf, pT[pp][:, 0:512], start=f2, stop=l2, tile_position=(0, 0)),
                    MM(banks["SUM"][32:33, :], ones1_bf, pT[pb][:, 0:512], start=f2, stop=l2, tile_position=(0, 32)),
                    MM(banks["SUM"][64:65, :], ones1_bf, pT[pp][:, 512:1024], start=f2, stop=l2,
                       tile_position=(0, 64)),
                    MM(banks["SUM"][96:97, :], ones1_bf, pT[pb][:, 512:1024], start=f2, stop=l2,
                       tile_position=(0, 96)),
                ]
                rk = rk + (pT_key[pp],)
                wk = ("O0", "O1", "SUM")
            T.group("pe", mms, rk, wk)
            if g - last_pop >= 6:
                if aux_live[0]:
                    pop_conv()
                    last_pop = g
                elif fq:
                    fq.pop(0)()
                    last_pop = g
                elif cq and g >= CONV_MIN_STEP:
                    pop_conv()
                    last_pop = g
            if last:
                while aux_live[0]:
                    pop_conv()
                while fq:
                    fq.pop(0)()
                s0, fst = finalize_stages(h)
                s0()
                fq = list(fst)
                last_pop = g
        state["item"] = item_base + 4 * nsb
        while aux_live[0]:
            pop_conv()
        while fq:
            fq.pop(0)()
        while cq:
            pop_conv()

        for k in range(8):
            bya, byc, bmc, bma = ("S0a", "S0b", "S1a", "S1b") if k % 2 == 0 else ("O0", "O1", "SUM", "AUX")
            T.group("pe", [MM(banks[bya], Wao[:, hh, k * 128:(k + 1) * 128], xb[1][:, 4 + hh, :],
                              start=(hh == 0), stop=(hh == 3)) for hh in range(4)],
                    ("Wao", "og0", "og1", "og2", "og3"), (bya,))
            T.group("pe", [MM(banks[byc], Wcp[:, j, k * 128:(k + 1) * 128], U2[:, j, :],
                              start=(j == 0), stop=(j == 3)) for j in range(4)],
                    ("Wcp", "u2_0", "u2_1", "u2_2", "u2_3"), (byc,))
            proj_group(bmc, R_MC + k * 128, lambda c: xb[0][:, c, :], "xall")
            proj_group(bma, R_MA + k * 128, lambda c: xb[0][:, c, :], "xall")
            s1, s2 = falloc(), falloc()
            T.op("act", ACTF(ft[s1][:, :], banks[bmc], AF.Sigmoid,
                             bias=csb[:, B_R + R_MC // 128 + k:B_R + R_MC // 128 + k + 1], scale=1.0),
                 (bmc, "csb"), (f"f{s1}",))
            T.op("act", ACTF(ft[s2][:, :], banks[bma], AF.Sigmoid,
                             bias=csb[:, B_R + R_MA // 128 + k:B_R + R_MA // 128 + k + 1], scale=1.0),
                 (bma, "csb"), (f"f{s2}",))
            T.op("dve", STT(ft[s1][:, :], banks[byc], csb[:, B_CP + k:B_CP + k + 1], ft[s1][:, :], ALU.add, ALU.mult),
                 (byc, "csb", f"f{s1}"), (f"f{s1}",))
            T.op("dve", TT(ft[s2][:, :], banks[bya], ft[s2][:, :], ALU.mult), (bya, f"f{s2}"), (f"f{s2}",))
            T.op("dve", TT(merged[k], ft[s1][:, :], ft[s2][:, :], ALU.add), (f"f{s1}", f"f{s2}"),
                 (merged_key[k], f"mg{k}"))
            ffree(s1, s2)
        if qbi + 1 < len(qb_list):
            emit_xload(qbi + 1, qb_list[qbi + 1][3])
            emit_glu()
        mkeys = tuple(f"mg{k}" for k in range(8))
        for j in range(4):
            zi = (qbi * 4 + j) % 2
            z = zbuf[zi]
            zk = f"z{zi}"
            xt0, xt1 = falloc(), falloc()
            T.dma("sp", f"df{xt0}", [DMA(ft[xt0][:, :], x_own[o0 + j * 128:o0 + (j + 1) * 128, 0:512])],
                  writes=(f"f{xt0}",))
            T.dma("sp", f"df{xt1}", [DMA(ft[xt1][:, :], x_own[o0 + j * 128:o0 + (j + 1) * 128, 512:1024])],
                  writes=(f"f{xt1}",))
            for n, xt in ((0, xt0), (1, xt1)):
                bk = (("S0a", "S0b"), ("S1a", "S1b"))[j % 2][n]
                T.group("pe", [MM(banks[bk], merged[k][:, j * 128:(j + 1) * 128], Wo[:, k, n * 512:(n + 1) * 512],
                                  start=(k == 0), stop=(k == 7)) for k in range(8)], ("Wo",) + mkeys + tuple(pT_key), (bk,))
                T.op("dve", TT(z[:, n * 512:(n + 1) * 512], banks[bk], bro[0][:, n * 512:(n + 1) * 512], ALU.add),
                     (bk, "bro"), (zk,) + zalias[zi])
                T.op("dve", STT(z[:, n * 512:(n + 1) * 512], ft[xt][:, :], ALPHA, z[:, n * 512:(n + 1) * 512],
                                 ALU.mult, ALU.add), (f"f{xt}", zk), (zk,))
            T.op("act", ACTF(z, z, AF.Identity, accum_out=small[:, 0:1]), (zk,), (zk, "small"))
            for n, xt in ((0, xt0), (1, xt1)):
                T.op("act", ACTF(ft[xt][:, :], z[:, n * 512:(n + 1) * 512], AF.Square, accum_out=small[:, 2 + n:3 + n]),
                     (zk,), (f"f{xt}", "small"))
            T.op("dve", TS(small[:, 1:2], small[:, 0:1], 1.0 / D, None, ALU.mult), ("small",), ("small",))
            T.op("dve", TT(small[:, 4:5], small[:, 2:3], small[:, 3:4], ALU.add), ("small",), ("small",))
            T.op("dve", TT(small[:, 6:7], small[:, 1:2], small[:, 1:2], ALU.mult), ("small",), ("small",))
            T.op("dve", STT(small[:, 4:5], small[:, 4:5], 1.0 / D, small[:, 6:7], ALU.mult, ALU.subtract),
                 ("small",), ("small",))
            T.op("act", ACTF(small[:, 4:5], small[:, 4:5], AF.Ln, bias=dyn[:, Y_EPS:Y_EPS + 1], scale=1.0),
                 ("small", "dyn"), ("small",))
            T.op("act", ACTF(small[:, 5:6], small[:, 4:5], AF.Exp, scale=-0.5), ("small",), ("small",))
            T.op("dve", STT(small[:, 7:8], small[:, 1:2], -1.0, small[:, 5:6], ALU.mult, ALU.mult), ("small",),
                 ("small",))
            T.op("act", ACTF(z, z, AF.Identity, bias=small[:, 7:8], scale=small[:, 5:6]), (zk, "small"), (zk,))
            T.op("dve", TT(z, z, bro[1][:, :], ALU.mult), (zk, "bro"), (zk,))
            T.op("pool", TT(z, z, bro[2][:, :], ALU.add), (zk, "bro"), (zk,))
            T.dma("pool", f"dout{zi}", [DMA(y_own[o0 + j * 128:o0 + (j + 1) * 128, :], z)], reads=(zk,),
                  writes=("yout",))
            ffree(xt0, xt1)

    T.barrier()
    with nc.Block() as block:
        T.replay(block)
    es.close()
    return nc


FULL_SEGS = [(16384, 2048), (8192, 1024), (8192, 1024)]


def rope_tables(npos):
    pos = np.arange(npos, dtype=np.float64)
    inv_freq = 1.0 / np.power(10000.0, np.arange(0, HD, 2, dtype=np.float64) / HD)
    ang = pos[:, None] * inv_freq[None, :]
    cos = np.cos(ang).astype(np.float32)
    sin = np.sin(ang).astype(np.float32)
    cos64 = np.concatenate([cos, cos], axis=1)
    sin64 = np.concatenate([-sin, sin], axis=1)
    cosT = np.ascontiguousarray(np.concatenate([cos64, cos64], axis=1).T)
    sinT = np.ascontiguousarray(np.concatenate([sin64, sin64], axis=1).T)
    return cosT, sinT


def prep_inputs(segs, xs, w):
    ncore = NCORES
    x_all = np.concatenate(xs, axis=0)
    NTOK = x_all.shape[0]
    xT_all = np.ascontiguousarray(x_all.T)
    SMAX = max(s for s, _ in segs)
    cosk, sink = rope_tables(SMAX)
    w_in = w["w_in"]
    b_in = w["b_in"]
    w_kv = np.ascontiguousarray(w_in[:, 2048:3072])
    w_r = np.ascontiguousarray(np.concatenate([w_in[:, :2048], w_in[:, 3072:]], axis=1))
    b_rest = np.concatenate([b_in[:2048], b_in[3072:]])
    b_r = np.ascontiguousarray(b_rest.reshape(36, 128).T)
    b_k = np.ascontiguousarray(b_in[2048:2560].reshape(4, 128).T)
    b_v = np.ascontiguousarray(b_in[2560:3072].reshape(4, 128).T)
    perm = np.zeros((128, 128), np.float32)
    for m in range(128):
        d = m % 64
        perm[m + 32 if d < 32 else m - 32, m] = 1.0
    convw = np.ascontiguousarray(w["conv_dw"].T.reshape(4, 128, CONVK).transpose(1, 0, 2).reshape(128, 4 * CONVK))
    cvec = np.concatenate([w["conv_dw_b"].reshape(4, 128).T, w["conv_ln_g"].reshape(4, 128).T,
                           w["conv_ln_b"].reshape(4, 128).T], axis=1)
    cv = np.zeros((128, 1024), np.float32)
    cv[:, 0:36] = b_r
    cv[:, 36:40] = b_k
    cv[:, 40:44] = b_v
    cv[:, 44:52] = w["b_conv_proj"].reshape(8, 128).T
    cv[:, 52] = w["subln_g"]
    cv[:, 56:60] = np.array([1.0, 0.0, 0.0, 1.0], np.float32)[None, :]
    cv[:, 64:188] = convw
    cv[:, 188:200] = cvec
    cv[:, 200:328] = 1.0
    cv[:, 328:456] = perm
    cv[0, 456:584] = 1.0
    cv[32, 456:584] = 1.0
    cv[64, 584:712] = 1.0
    cv[96, 584:712] = 1.0
    cv[:, 712 + 64] = 1.0
    lamv = np.concatenate([w["lam_q1"], w["lam_k1"], w["lam_q2"], w["lam_k2"]]).reshape(1, 256).astype(np.float32)
    rowv = np.ascontiguousarray(np.stack([w["b_out"], w["ln_g"], w["ln_b"]], axis=0))
    shared = dict(xT_all=xT_all, w_kv=w_kv, w_r=w_r, cvals=cv, cosk=cosk, sink=sink,
                  w_cp=np.ascontiguousarray(w["w_conv_proj"]), lamv=lamv,
                  w_ao=np.ascontiguousarray(w["w_attn_o"]), w_o=np.ascontiguousarray(w["w_out"]), rowv=rowv)
    maps = []
    for c in range(ncore):
        xo_cols, xown_rows, cq, sq, hms = [], [], [], [], []
        for (S, nq), x in zip(segs, xs):
            a = c * nq
            blk = np.zeros((nq + 2 * HALO, D), np.float32)
            lo, hi = a - HALO, a + nq + HALO
            slo, shi = max(lo, 0), min(hi, S)
            blk[slo - lo:shi - lo] = x[slo:shi]
            xo_cols.append(blk.T)
            xown_rows.append(x[a:a + nq])
            cq.append(cosk[:, a:a + nq])
            sq.append(sink[:, a:a + nq])
            for qb in range(nq // 512):
                q0 = a + qb * 512
                idx = np.concatenate([np.arange(q0 - HALO, q0), np.arange(q0 + 512, q0 + 512 + HALO)])
                valid = ((idx >= 0) & (idx < S)).astype(np.float32)
                hms.append(np.broadcast_to(valid[None, :], (128, 32)))
        m = dict(shared)
        m["xT_own"] = np.ascontiguousarray(np.concatenate(xo_cols, axis=1))
        m["x_own"] = np.ascontiguousarray(np.concatenate(xown_rows, axis=0))
        m["cosq"] = np.ascontiguousarray(np.concatenate(cq, axis=1))
        m["sinq"] = np.ascontiguousarray(np.concatenate(sq, axis=1))
        m["hmask"] = np.ascontiguousarray(np.concatenate(hms, axis=1))
        maps.append(m)
    return maps


_PROG_CACHE = {}


def run_layer(segs, xs, w, trace=False):
    maps = prep_inputs(segs, xs, w)
    key = tuple(segs)
    if key not in _PROG_CACHE:
        _PROG_CACHE[key] = build_program(segs)
    nc = _PROG_CACHE[key]
    res = run_bass_kernel_spmd(nc, maps, core_ids=list(range(NCORES)), **({"trace": True} if trace else {}))
    outs = []
    off = 0
    for S, nq in segs:
        y = np.empty((S, D), np.float32)
        for c in range(NCORES):
            y[c * nq:(c + 1) * nq] = res.results[c]["y_own"][off:off + nq]
        outs.append(y)
        off += nq
    return outs, res


def kernel(x_prompt, x_sample, w_in, b_in, conv_dw, conv_dw_b, conv_ln_g, conv_ln_b, w_conv_proj, b_conv_proj,
           lam_q1, lam_k1, lam_q2, lam_k2, subln_g, w_attn_o, w_out, b_out, ln_g, ln_b):
    f = lambda a: np.ascontiguousarray(np.asarray(a, dtype=np.float32))
    w = dict(w_in=f(w_in)[0], b_in=f(b_in)[0], conv_dw=f(conv_dw)[0], conv_dw_b=f(conv_dw_b)[0],
             conv_ln_g=f(conv_ln_g)[0], conv_ln_b=f(conv_ln_b)[0], w_conv_proj=f(w_conv_proj)[0],
             b_conv_proj=f(b_conv_proj)[0], lam_q1=f(lam_q1)[0], lam_k1=f(lam_k1)[0], lam_q2=f(lam_q2)[0],
             lam_k2=f(lam_k2)[0], subln_g=f(subln_g)[0], w_attn_o=f(w_attn_o)[0], w_out=f(w_out)[0],
             b_out=f(b_out)[0], ln_g=f(ln_g)[0], ln_b=f(ln_b)[0])
    xp = f(x_prompt)
    xsm = f(x_sample)
    xs = [xp[0], xsm[0], xsm[1]]
    outs, _ = run_layer(FULL_SEGS, xs, w)
    y_prompt = outs[0][None]
    y_sample = np.stack([outs[1], outs[2]], axis=0)
    return (y_prompt, y_sample)
```

```python
import numpy as np
from contextlib import ExitStack
import concourse.bass as bass
import concourse.mybir as mybir
from concourse.bass_utils import run_bass_kernel_spmd

F32 = mybir.dt.float32
BF16 = mybir.dt.bfloat16
AF = mybir.ActivationFunctionType
ALU = mybir.AluOpType

NCORES = 8
D = 1024
HD = 64
NH = 4
CONVK = 31
LN_EPS = 1e-5
DEPTH = 1
ALPHA = (2.0 * DEPTH) ** 0.25
LAM_INIT = 0.8 - 0.6 * 1.0
R_GA, R_GB, R_CG, R_Q, R_AG, R_MC, R_MA = 0, 512, 1024, 1536, 2048, 2560, 3584
NR = 4608
HALO = 16

ENGS = ("pe", "act", "dve", "pool", "sp")


class Tracker:
    def __init__(self, nc, es):
        self.nc = nc
        self.es = es
        self.prog = {e: [] for e in ENGS}
        self.semobj = {}
        self.semval = {}
        self.waited = {e: {} for e in ENGS}
        self.lastw = {}
        self.readers = {}
        for e in ("pe", "act", "dve", "pool"):
            self.new_sem(e)

    def new_sem(self, name):
        self.semobj[name] = self.es.enter_context(self.nc.semaphore("s_" + name))
        self.semval[name] = 0
        return name

    PSUM_KEYS = frozenset(("S0a", "S0b", "S1a", "S1b", "O0", "O1", "SUM", "AUX"))

    def _excl(self, reads, writes):
        r = tuple(k for k in reads if k not in self.PSUM_KEYS)
        w = tuple(writes) + tuple(k for k in reads if k in self.PSUM_KEYS)
        return r, w

    def _deps(self, reads, writes):
        deps = {}

        def add(tok):
            if tok is not None:
                deps[tok[0]] = max(deps.get(tok[0], 0), tok[1])

        for k in reads:
            add(self.lastw.get(k))
        for k in writes:
            add(self.lastw.get(k))
            for s, v in self.readers.get(k, {}).items():
                add((s, v))
        return deps

    def _waits(self, eng, deps):
        for s, v in deps.items():
            if s == "pe" and eng == "pe":
                continue
            if self.waited[eng].get(s, 0) >= v:
                continue
            self.waited[eng][s] = v
            so = self.semobj[s]
            self.prog[eng].append(lambda e, so=so, v=v: e.wait_ge(so, v))

    def _record(self, tok, reads, writes):
        for k in reads:
            r = self.readers.setdefault(k, {})
            r[tok[0]] = max(r.get(tok[0], 0), tok[1])
        for k in writes:
            self.lastw[k] = tok
            self.readers[k] = {}

    def op(self, eng, fn, reads=(), writes=()):
        reads, writes = self._excl(reads, writes)
        self._waits(eng, self._deps(reads, writes))
        self.semval[eng] += 1
        so = self.semobj[eng]
        self.prog[eng].append(lambda e, fn=fn, so=so: fn(e).then_inc(so, 1))
        tok = (eng, self.semval[eng])
        self._record(tok, reads, writes)
        return tok

    def group(self, eng, fns, reads=(), writes=()):
        reads, writes = self._excl(reads, writes)
        self._waits(eng, self._deps(reads, writes))
        self.semval[eng] += 1
        so = self.semobj[eng]
        for fn in fns[:-1]:
            self.prog[eng].append(lambda e, fn=fn: fn(e))
        self.prog[eng].append(lambda e, fn=fns[-1], so=so: fn(e).then_inc(so, 1))
        tok = (eng, self.semval[eng])
        self._record(tok, reads, writes)
        return tok

    def dma(self, eng, sem, fns, reads=(), writes=()):
        self._waits(eng, self._deps(reads, writes))
        so = self.semobj[sem]
        for fn in fns:
            self.semval[sem] += 16
            self.prog[eng].append(lambda e, fn=fn, so=so: fn(e).then_inc(so, 16))
        tok = (sem, self.semval[sem])
        self._record(tok, reads, writes)
        return tok

    def barrier(self):
        deps = {s: v for s, v in self.semval.items() if v > 0}
        for e in ENGS:
            self._waits(e, dict(deps))

    def replay(self, block):
        prog = self.prog

        @block.tensor
        def _(e):
            for f in prog["pe"]:
                f(e)

        @block.scalar
        def _(e):
            for f in prog["act"]:
                f(e)

        @block.vector
        def _(e):
            for f in prog["dve"]:
                f(e)

        @block.gpsimd
        def _(e):
            for f in prog["pool"]:
                f(e)

        @block.sync
        def _(e):
            for f in prog["sp"]:
                f(e)


def MM(out, lhsT, rhs, start=True, stop=True, **kw):
    return lambda g: g.matmul(out, lhsT=lhsT, rhs=rhs, start=start, stop=stop, **kw)


def DMA(out, in_):
    return lambda g: g.dma_start(out=out, in_=in_)


def ACTF(out, in_, func, **kw):
    return lambda g: g.activation(out=out, in_=in_, func=func, **kw)


def TT(out, in0, in1, op):
    return lambda g: g.tensor_tensor(out=out, in0=in0, in1=in1, op=op)


def STT(out, in0, scalar, in1, op0, op1):
    return lambda g: g.scalar_tensor_tensor(out=out, in0=in0, scalar=scalar, in1=in1, op0=op0, op1=op1)


def TS(out, in0, s1, s2, op0, op1=None):
    if op1 is None:
        return lambda g: g.tensor_scalar(out=out, in0=in0, scalar1=s1, scalar2=None, op0=op0)
    return lambda g: g.tensor_scalar(out=out, in0=in0, scalar1=s1, scalar2=s2, op0=op0, op1=op1)


def CP(out, in_):
    return lambda g: g.tensor_copy(out=out, in_=in_)


def RED(out, in_, op):
    return lambda g: g.tensor_reduce(out=out, in_=in_, axis=mybir.AxisListType.X, op=op)


def MEMSET(ap, v):
    return lambda g: g.memset(ap, v)


def RECIP(out, in_):
    return lambda g: g.reciprocal(out=out, in_=in_)


def build_program(segs, debug=False, stage=99):
    NTOK = sum(s for s, _ in segs)
    NOWN = sum(q for _, q in segs)
    SMAX = max(s for s, _ in segs)
    QB = 512
    XO = sum(q + 2 * HALO for _, q in segs)
    NQB = NOWN // QB

    nc = bass.Bass("TRN2", target_bir_lowering=False)
    es = ExitStack()

    def din(name, shape, dt=F32):
        return nc.dram_tensor(name, list(shape), dt, kind="ExternalInput").ap()

    xT_all = din("xT_all", [D, NTOK])
    xT_own = din("xT_own", [D, XO])
    x_own = din("x_own", [NOWN, D])
    hmask = din("hmask", [128, NQB * 32])
    w_kv = din("w_kv", [D, 1024])
    w_r = din("w_r", [D, NR])
    cvals = din("cvals", [128, 1024])
    cosk = din("cosk", [128, SMAX])
    sink = din("sink", [128, SMAX])
    cosq = din("cosq", [128, NOWN])
    sinq = din("sinq", [128, NOWN])
    w_cp = din("w_cp", [512, D])
    lamv = din("lamv", [1, 256])
    w_ao = din("w_ao", [512, D])
    w_o = din("w_o", [D, D])
    rowv = din("rowv", [3, D])
    y_own = nc.dram_tensor("y_own", [NOWN, D], F32, kind="ExternalOutput").ap()
    skind = "ExternalOutput" if debug else "Internal"
    kT_scr = nc.dram_tensor("kT_scr", [NH, 128, NTOK], BF16, kind=skind).ap()
    v_scr = nc.dram_tensor("v_scr", [NH, 128, NTOK // 128, 128], BF16, kind=skind).ap()

    def sb(name, shape, dt):
        return es.enter_context(nc.sbuf_tensor(name, list(shape), dt))

    Wkv = sb("Wkv", [128, 8, 1024], BF16)
    Wr = sb("Wr", [128, 8, NR], BF16)
    Wcp = sb("Wcp", [128, 4, D], BF16)
    Wao = sb("Wao", [128, 4, D], BF16)
    Wo = sb("Wo", [128, 8, D], BF16)
    NF = 13
    ft = [sb(f"f{i}", [128, 512], F32) for i in range(NF)]
    xb = [sb(f"xb{i}", [128, 8, 512], BF16) for i in range(2)]
    kbf = [sb(f"kbf{i}", [128, 512], BF16) for i in range(2)]
    kTo = [sb(f"kTo{i}", [128, 4, 512], BF16) for i in range(2)]
    vo = [sb(f"vo{i}", [128, 2048], BF16) for i in range(2)]
    uin = sb("uin", [128, 4, 512 + 2 * HALO], F32)
    bro = [sb(f"bro{i}", [128, D], F32) for i in range(3)]
    xhb = sb("xhb", [128, 8, 2 * HALO], BF16)
    csb = sb("csb", [128, 1024], F32)
    dyn = sb("dyn", [128, 16], F32)
    cbf = sb("cbf", [128, 264], BF16)
    hm = sb("hm", [128, 32], F32)
    lamt = sb("lamt", [1, 256], F32)
    small = sb("small", [128, 8], F32)
    B_R, B_K, B_V, B_CP, C_SUBG = 0, 36, 40, 44, 52
    C_SEL, C_CW, C_CV, C_ONES, C_PERM, C_BSEL = 56, 64, 188, 200, 328, 456
    Y_NEGLAM, Y_OML, Y_GS, Y_CB, Y_EPS = 0, 1, 2, 4, 8
    perm_bf = cbf[:, 0:128]
    selA_bf, selB_bf = cbf[:, 256:258], cbf[:, 258:260]
    selA_f, selB_f = csb[:, C_SEL:C_SEL + 2], csb[:, C_SEL + 2:C_SEL + 4]
    ones_row_f = csb[0:1, C_ONES:C_ONES + 128]
    ones_col_f = csb[:, C_ONES:C_ONES + 1]
    indA_f = csb[:, C_BSEL:C_BSEL + 128]
    indB_f = csb[:, C_BSEL + 128:C_BSEL + 256]
    C_Z = 712
    zrow0_f = csb[:, C_Z + 64:C_Z + 192]
    zrow64_f = csb[:, C_Z:C_Z + 128]
    ones1_bf = cbf[:, 128:129]
    pS = [es.enter_context(nc.psum_tensor(f"pS{i}", [128, 1024], F32)) for i in range(2)]
    pOt = [es.enter_context(nc.psum_tensor(f"pO{i}", [128, 512], F32)) for i in range(2)]
    pSum = es.enter_context(nc.psum_tensor("pSum", [128, 512], F32))
    pAux = es.enter_context(nc.psum_tensor("pAux", [128, 512], F32))
    banks = {"S0a": pS[0][:, 0:512], "S0b": pS[0][:, 512:1024], "S1a": pS[1][:, 0:512],
             "S1b": pS[1][:, 512:1024], "O0": pOt[0][:, :], "O1": pOt[1][:, :], "SUM": pSum[:, :],
             "AUX": pAux[:, :]}

    T = Tracker(nc, es)
    for i in range(NF):
        T.new_sem(f"df{i}")
    for i in range(8):
        T.new_sem(f"dkv{i}")
    for n in ("dkTo0", "dkTo1", "dvo0", "dvo1", "dmisc", "dout0", "dout1", "dhm"):
        T.new_sem(n)

    free_f = list(range(NF))
    free_x = []

    def falloc():
        return free_f.pop(0)

    def falloc_x():
        return free_x.pop(0)

    def ffree(*idx):
        for i in idx:
            free_f.append(i)

    rr = {"cast": 0}

    def cast_op(out_ap, in_ap, reads, writes, engs=("dve", "pool", "act")):
        e = engs[rr["cast"] % len(engs)]
        rr["cast"] += 1
        if e == "act":
            return T.op("act", ACTF(out_ap, in_ap, AF.Copy), reads, writes)
        return T.op(e, CP(out_ap, in_ap), reads, writes)

    def finish():
        T.barrier()
        with nc.Block() as block:
            T.replay(block)
        es.close()
        return nc

    T.dma("sp", "dmisc", [
        DMA(csb[:, :], cvals[:, :]),
        DMA(lamt[:, :], lamv[:, :]),
        DMA(bro[0][:, :], rowv[0:1, :].partition_broadcast(128)),
        DMA(bro[1][:, :], rowv[1:2, :].partition_broadcast(128)),
        DMA(bro[2][:, :], rowv[2:3, :].partition_broadcast(128)),
    ], writes=("csb", "lamt", "bro"))
    for _b in banks:
        T.op("dve", MEMSET(banks[_b], 0.0), (), (_b,))
    if stage == 0:
        return finish()
    T.op("dve", MEMSET(dyn[:, Y_EPS:Y_EPS + 1], LN_EPS), (), ("dyn",))
    T.op("dve", CP(cbf[:, 0:128], csb[:, C_PERM:C_PERM + 128]), ("csb",), ("cbf",))
    T.op("dve", CP(cbf[:, 128:256], csb[:, C_ONES:C_ONES + 128]), ("csb",), ("cbf",))
    T.op("dve", CP(cbf[:, 256:260], csb[:, C_SEL:C_SEL + 4]), ("csb",), ("cbf",))
    T.op("dve", TT(lamt[:, 0:64], lamt[:, 0:64], lamt[:, 64:128], ALU.mult), ("lamt",), ("lamt",))
    T.op("dve", TT(lamt[:, 128:192], lamt[:, 128:192], lamt[:, 192:256], ALU.mult), ("lamt",), ("lamt",))
    T.op("dve", RED(lamt[:, 64:65], lamt[:, 0:64], ALU.add), ("lamt",), ("lamt",))
    T.op("dve", RED(lamt[:, 65:66], lamt[:, 128:192], ALU.add), ("lamt",), ("lamt",))
    T.op("act", ACTF(lamt[:, 66:68], lamt[:, 64:66], AF.Exp), ("lamt",), ("lamt",))
    T.op("dve", TT(lamt[:, 68:69], lamt[:, 66:67], lamt[:, 67:68], ALU.subtract), ("lamt",), ("lamt",))
    T.op("dve", TS(lamt[:, 70:71], lamt[:, 68:69], -1.0, -LAM_INIT, ALU.mult, ALU.add), ("lamt",), ("lamt",))
    T.op("dve", TS(lamt[:, 71:72], lamt[:, 68:69], -1.0, 1.0 - LAM_INIT, ALU.mult, ALU.add), ("lamt",), ("lamt",))
    T.group("pe", [MM(banks["AUX"][:, 0:2], ones_row_f, lamt[0:1, 70:72])], ("csb", "lamt"), ("AUX",))
    T.op("dve", CP(dyn[:, Y_NEGLAM:Y_NEGLAM + 2], banks["AUX"][:, 0:2]), ("AUX",), ("dyn",))
    T.op("dve", TS(dyn[:, Y_GS:Y_GS + 1], csb[:, C_SUBG:C_SUBG + 1], 1.0 - LAM_INIT, None, ALU.mult),
         ("csb", "dyn"), ("dyn",))
    T.op("dve", TS(dyn[:, Y_CB:Y_CB + 4], csb[:, B_V:B_V + 4], dyn[:, Y_OML:Y_OML + 1], None, ALU.mult),
         ("csb", "dyn"), ("dyn",))
    T.op("dve", TS(dyn[:, 9:13], csb[:, B_R + R_CG // 128:B_R + R_CG // 128 + 4], -1.0, None, ALU.mult),
         ("csb", "dyn"), ("dyn",))

    if stage == 1:
        return finish()

    wq = []

    def load_weight(dst, src, nrow_chunks, ncols, key, defer=False):
        for c in range(nrow_chunks):
            for g0 in range(0, ncols, 512):
                item = (dst[:, c, g0:g0 + 512], src[c * 128:(c + 1) * 128, g0:g0 + 512], key)
                if defer:
                    wq.append(item)
                else:
                    emit_wload(item, ("dve", "pool", "act"))

    def emit_wload(item, engs, ring=False):
        dst_ap, src_ap, key = item
        fi = falloc_x() if ring else falloc()
        T.dma("sp", f"df{fi}", [DMA(ft[fi][:, :], src_ap)], writes=(f"f{fi}",))
        cast_op(dst_ap, ft[fi][:, :], (f"f{fi}",), (key,), engs=engs)
        if ring:
            free_x.append(fi)
        else:
            ffree(fi)

    load_weight(Wkv, w_kv, 8, 1024, "Wkv")
    load_weight(Wr, w_r, 8, NR, "Wr", defer=True)
    load_weight(Wcp, w_cp, 4, D, "Wcp", defer=True)
    load_weight(Wao, w_ao, 4, D, "Wao", defer=True)
    load_weight(Wo, w_o, 8, D, "Wo", defer=True)

    if stage == 2:
        return finish()

    def rope_chunk(src_key, src_ps, bias_col, cos_ap, sin_ap, cos_key, sin_key, out_ap, out_key, rot_bank, kb_i,
                   final_eng="pool"):
        kb = kbf[kb_i]
        kbk = f"kbf{kb_i}"
        T.op("act", ACTF(kb[:, :], src_ps, AF.Identity, bias=bias_col, scale=1.0), (src_key, "csb"), (kbk,))
        T.group("pe", [MM(banks[rot_bank], perm_bf, kb[:, :])], (kbk, "cbf"), (rot_bank,))
        a = falloc()
        b = falloc()
        T.op("dve", STT(ft[a][:, :], src_ps, bias_col, cos_ap, ALU.add, ALU.mult), (src_key, "csb", cos_key),
             (f"f{a}",))
        T.op("dve", TT(ft[b][:, :], banks[rot_bank], sin_ap, ALU.mult), (rot_bank, sin_key), (f"f{b}",))
        T.op(final_eng, TT(out_ap, ft[a][:, :], ft[b][:, :], ALU.add), (f"f{a}", f"f{b}"), (out_key,))
        ffree(a, b)

    seg_tok0 = []
    t0 = 0
    for S, nq in segs:
        seg_tok0.append(t0)
        t0 += S
    pend_st = []
    for _ in range(8):
        free_x.append(free_f.pop())
    for t in range(NTOK // 512):
        tok0 = t * 512
        si = max(i for i in range(len(segs)) if seg_tok0[i] <= tok0)
        pos0 = tok0 - seg_tok0[si]
        xbi = t % 2
        xbk = tuple(f"xb{xbi}_{c}" for c in range(8))
        for c in range(8):
            fi = falloc_x()
            T.dma("sp", f"df{fi}", [DMA(ft[fi][:, :], xT_all[c * 128:(c + 1) * 128, tok0:tok0 + 512])],
                  writes=(f"f{fi}",))
            cast_op(xb[xbi][:, c, :], ft[fi][:, :], (f"f{fi}",), (xbk[c],), engs=("act",))
            free_x.append(fi)
        for _ in range(2):
            if wq:
                emit_wload(wq.pop(0), ("dve",), ring=True)
        ci = falloc()
        sii = falloc()
        T.dma("sp", f"df{ci}", [DMA(ft[ci][:, :], cosk[:, pos0:pos0 + 512])], writes=(f"f{ci}",))
        T.dma("sp", f"df{sii}", [DMA(ft[sii][:, :], sink[:, pos0:pos0 + 512])], writes=(f"f{sii}",))
        while pend_st:
            pend_st.pop(0)()
        ko = t % 2
        KB = ("S0a", "S0b", "S1a", "S1b")
        for h in range(4):
            T.group("pe", [MM(banks[KB[h]], Wkv[:, c, h * 128:(h + 1) * 128], xb[xbi][:, c, :],
                              start=(c == 0), stop=(c == 7)) for c in range(8)], ("Wkv",) + xbk, (KB[h],))
        vov = vo[ko][:, :].rearrange("p (h j e) -> p h j e", h=4, j=4)
        for i in range(4):
            rope_chunk(KB[i], banks[KB[i]], csb[:, B_K + i:B_K + i + 1], ft[ci][:, :], ft[sii][:, :],
                       f"f{ci}", f"f{sii}", kTo[ko][:, i, :], f"kTo{ko}", ("SUM", "AUX")[i % 2], i % 2,
                       final_eng="dve")
            vbank = ("O0", "O1")[i % 2]
            T.group("pe", [MM(banks[vbank], xb[xbi][:, c, i * 128:(i + 1) * 128], Wkv[:, c, 512:1024],
                              start=(c == 0), stop=(c == 7)) for c in range(8)], ("Wkv",) + xbk, (vbank,))
            T.op("dve", CP(vov[:, :, i, :], banks[vbank].rearrange("p (h e) -> p h e", h=4)),
                 (vbank,), (f"vo{ko}",))
        ffree(ci, sii)
        J0 = tok0 // 128

        def stores(ko=ko, tok0=tok0, J0=J0):
            T.dma("sp", f"dkTo{ko}", [DMA(kT_scr[:, :, tok0:tok0 + 512].rearrange("h p t -> p h t"),
                                          kTo[ko][:, :, :])], reads=(f"kTo{ko}",), writes=("scr",))
            T.dma("sp", f"dvo{ko}", [DMA(v_scr[:, :, J0:J0 + 4, :].rearrange("h p j e -> p h (j e)"),
                                         vo[ko][:, :].rearrange("p (h je) -> p h je", h=4))],
                  reads=(f"vo{ko}",), writes=("scr",))
        pend_st.append(stores)
    while pend_st:
        pend_st.pop(0)()
    while free_x:
        free_f.append(free_x.pop(0))
    while wq:
        emit_wload(wq.pop(0), ("dve", "pool", "act"))

    if stage == 3:
        return finish()
    T.barrier()

    QT, U2 = kTo[0], kTo[1]
    xb1f = xb[1][:, :, :].rearrange("p a b -> p (a b)").bitcast(F32)
    zbuf = [xb1f[:, 0:1024], xb1f[:, 1024:2048]]
    zalias = [tuple(f"ags{h}" for h in range(4)), tuple(f"og{h}" for h in range(4))]
    pT = [vo[0][:, 0:1024], vo[0][:, 1024:2048], vo[1][:, 0:1024], vo[1][:, 1024:2048]]
    pT_key = ["vo0a", "vo0b", "vo1a", "vo1b"]
    merged = [vo[k // 4][:, (k % 4) * 512:(k % 4 + 1) * 512] for k in range(8)]
    merged_key = [pT_key[k // 2] for k in range(8)]

    XALL = tuple(f"x0_{c}" for c in range(8))

    def proj_group(bank, col0, rhs_fn, rkey, n=8, width=None):
        rkeys = XALL if rkey == "xall" else (rkey,)
        out = banks[bank] if width is None else banks[bank][:, 0:width]
        return T.group("pe", [MM(out, Wr[:, c, col0:col0 + 128], rhs_fn(c), start=(c == 0), stop=(c == n - 1))
                              for c in range(n)], ("Wr",) + rkeys, (bank,))

    items = []
    qb_list = []
    own0 = 0
    xo0 = 0
    for si, (S, nq) in enumerate(segs):
        for qb in range(nq // QB):
            qb_list.append((si, S, own0 + qb * QB, xo0 + qb * QB))
            for h in range(4):
                for sbk in range(S // 512):
                    items.append((si, h, sbk))
        own0 += nq
        xo0 += nq + 2 * HALO
    PF = 6
    state = {"next_dma": 0, "step": 0}

    def issue_kv(n):
        si, h, sbk = items[n]
        slot = n % 8
        tok0 = seg_tok0[si] + sbk * 512
        J0 = tok0 // 128
        T.dma("sp", f"dkv{slot}", [
            DMA(Wkv[:, slot, 0:512], kT_scr[h, :, tok0:tok0 + 512]),
            DMA(Wkv[:, slot, 512:1024].rearrange("p (j e) -> p j e", j=4), v_scr[h, :, J0:J0 + 4, :]),
        ], reads=("scr",), writes=(f"kv{slot}",))

    def finalize_stages(h):
        o1, o2, stl, r1, r2 = falloc(), falloc(), falloc(), falloc(), falloc()

        def s0():
            T.op("dve", CP(ft[o1][:, :], banks["O0"]), ("O0",), (f"f{o1}",))
            T.op("dve", CP(ft[o2][:, :], banks["O1"]), ("O1",), (f"f{o2}",))
            T.op("dve", CP(ft[stl][:, :], banks["SUM"]), ("SUM",), (f"f{stl}",))

        def s1():
            T.group("pe", [MM(banks["AUX"], indA_f, ft[stl][:, :])], ("csb", f"f{stl}"), ("AUX",))
            T.op("dve", RECIP(ft[r1][:, :], banks["AUX"]), ("AUX",), (f"f{r1}",))

        def s2():
            T.group("pe", [MM(banks["AUX"], indB_f, ft[stl][:, :])], ("csb", f"f{stl}"), ("AUX",))
            T.op("dve", RECIP(ft[r2][:, :], banks["AUX"]), ("AUX",), (f"f{r2}",))
            ffree(stl)
            T.op("dve", TT(ft[o1][:, :], ft[o1][:, :], ft[r1][:, :], ALU.mult), (f"f{o1}", f"f{r1}"), (f"f{o1}",))
            T.op("dve", TT(ft[o2][:, :], ft[o2][:, :], ft[r2][:, :], ALU.mult), (f"f{o2}", f"f{r2}"), (f"f{o2}",))
            T.op("dve", STT(ft[o1][:, :], ft[o2][:, :], dyn[:, Y_NEGLAM:Y_NEGLAM + 1], ft[o1][:, :], ALU.mult,
                            ALU.add), (f"f{o1}", f"f{o2}", "dyn"), (f"f{o1}",))
            T.op("dve", TS(ft[o1][:, :], ft[o1][:, :], dyn[:, Y_CB + h:Y_CB + h + 1], None, ALU.add),
                 (f"f{o1}", "dyn"), (f"f{o1}",))

        def s3():
            T.op("dve", TT(ft[o2][:, :], ft[o1][:, :], ft[o1][:, :], ALU.mult), (f"f{o1}",), (f"f{o2}",))
            ffree(r2)

        def s4():
            T.group("pe", [MM(banks["AUX"][0:1, :], ones_col_f, ft[o2][:, :])], ("csb", f"f{o2}"), ("AUX",))
            T.op("dve", CP(ft[r1][0:1, :], banks["AUX"][0:1, :]), ("AUX",), (f"f{r1}",))

        def s5():
            T.group("pe", [MM(banks["AUX"], ones_row_f, ft[r1][0:1, :])], ("csb", f"f{r1}"), ("AUX",))
            ffree(r1)

        def s6():
            T.op("act", ACTF(ft[o2][:, :], banks["AUX"], AF.Ln, bias=dyn[:, Y_EPS:Y_EPS + 1], scale=1.0 / 128.0),
                 ("AUX", "dyn"), (f"f{o2}",))
            T.op("act", ACTF(ft[o2][:, :], ft[o2][:, :], AF.Exp, scale=-0.5), (f"f{o2}",), (f"f{o2}",))
            T.op("dve", STT(ft[o1][:, :], ft[o1][:, :], dyn[:, Y_GS:Y_GS + 1], ft[o2][:, :], ALU.mult, ALU.mult),
                 (f"f{o1}", f"f{o2}", "dyn"), (f"f{o1}",))
            T.op("dve", TT(xb[1][:, 4 + h, :], ft[o1][:, :], xb[1][:, h, :], ALU.mult),
                 (f"f{o1}", f"ags{h}"), (f"og{h}", "z1"))
            ffree(o1, o2)

        return s0, [s1, s2, s3, s4, s5, s6]

    def run_gen(g):
        for _ in g:
            pass

    def emit_xload(qbi_, xc0_):
        for c in range(8):
            fi = falloc()
            T.dma("sp", f"df{fi}", [DMA(ft[fi][:, :], xT_own[c * 128:(c + 1) * 128, xc0_ + HALO:xc0_ + HALO + 512])],
                  writes=(f"f{fi}",))
            cast_op(xb[0][:, c, :], ft[fi][:, :], (f"f{fi}",), (XALL[c],), engs=("dve", "pool"))
            ffree(fi)
        fi = falloc()
        hv = ft[fi][:, 0:256].rearrange("p (c s w) -> p c s w", c=8, s=2)
        T.dma("sp", f"df{fi}", [
            DMA(hv[:, :, 0, :], xT_own[:, xc0_:xc0_ + HALO].rearrange("(c p) w -> p c w", p=128)),
            DMA(hv[:, :, 1, :], xT_own[:, xc0_ + HALO + 512:xc0_ + 2 * HALO + 512].rearrange(
                "(c p) w -> p c w", p=128)),
        ], writes=(f"f{fi}",))
        T.op("dve", CP(xhb[:, :, :].rearrange("p c w -> p (c w)"), ft[fi][:, 0:256]), (f"f{fi}",), ("xhb",))
        ffree(fi)
        T.dma("sp", "dhm", [DMA(hm[:, :], hmask[:, qbi_ * 32:(qbi_ + 1) * 32])], writes=("hm",))


    def emit_glu():
        for j in range(4):
            ba, bb = ("O0", "O1", "SUM")[(2 * j) % 3], ("O0", "O1", "SUM")[(2 * j + 1) % 3]
            proj_group(ba, R_GA + j * 128, lambda c: xb[0][:, c, :], "xall")
            proj_group(bb, R_GB + j * 128, lambda c: xb[0][:, c, :], "xall")
            sg = falloc()
            bga = csb[:, B_R + (R_GA // 128) + j:B_R + (R_GA // 128) + j + 1]
            bgb = csb[:, B_R + (R_GB // 128) + j:B_R + (R_GB // 128) + j + 1]
            T.op("act", ACTF(ft[sg][:, :], banks[bb], AF.Sigmoid, bias=bgb, scale=1.0), (bb, "csb"), (f"f{sg}",))
            T.op("dve", STT(uin[:, j, HALO:HALO + 512], banks[ba], bga, ft[sg][:, :], ALU.add, ALU.mult),
                 (ba, "csb", f"f{sg}"), (f"uin{j}",))
            ha, hb = (("S0a", "S0b"), ("S1a", "S1b"))[j % 2]
            proj_group(ha, R_GA + j * 128, lambda c: xhb[:, c, :], "xhb", width=32)
            T.op("dve", TS(ft[sg][:, 32:64], banks[ha][:, 0:32], bga, None, ALU.add), (ha, "csb"),
                 (f"f{sg}",))
            proj_group(hb, R_GB + j * 128, lambda c: xhb[:, c, :], "xhb", width=32)
            T.op("act", ACTF(ft[sg][:, 0:32], banks[hb][:, 0:32], AF.Sigmoid, bias=bgb, scale=1.0),
                 (hb, "csb"), (f"f{sg}",))
            T.op("dve", TT(ft[sg][:, 0:32], ft[sg][:, 0:32], ft[sg][:, 32:64], ALU.mult), (f"f{sg}",), (f"f{sg}",))
            T.op("dve", TT(uin[:, j, 0:HALO], ft[sg][:, 0:HALO], hm[:, 0:HALO], ALU.mult), (f"f{sg}", "hm"),
                 (f"uin{j}",))
            T.op("dve", TT(uin[:, j, HALO + 512:2 * HALO + 512], ft[sg][:, HALO:2 * HALO], hm[:, HALO:2 * HALO],
                           ALU.mult), (f"f{sg}", "hm"), (f"uin{j}",))
            ffree(sg)


    for qbi, (si, S, o0, xc0) in enumerate(qb_list):
        xk = tuple(f"x0_{c}" for c in range(8))
        if qbi == 0:
            emit_xload(qbi, xc0)
            emit_glu()

        cq, sq_ = falloc(), falloc()
        T.dma("sp", f"df{cq}", [DMA(ft[cq][:, :], cosq[:, o0:o0 + 512])], writes=(f"f{cq}",))
        T.dma("sp", f"df{sq_}", [DMA(ft[sq_][:, :], sinq[:, o0:o0 + 512])], writes=(f"f{sq_}",))
        for h in range(4):
            qbank = ("S0a", "S0b")[h % 2]
            rbank = ("S1a", "S1b")[h % 2]
            proj_group(qbank, R_Q + h * 128, lambda c: xb[0][:, c, :], "xall")
            rope_chunk(qbank, banks[qbank], csb[:, B_R + R_Q // 128 + h:B_R + R_Q // 128 + h + 1], ft[cq][:, :],
                       ft[sq_][:, :], f"f{cq}", f"f{sq_}", QT[:, h, :], f"qt{h}", rbank, h % 2)
        ffree(cq, sq_)
        for h in range(4):
            gbank = ("O0", "O1")[h % 2]
            proj_group(gbank, R_AG + h * 128, lambda c: xb[0][:, c, :], "xall")
            T.op("act", ACTF(xb[1][:, h, :], banks[gbank], AF.Silu,
                             bias=csb[:, B_R + R_AG // 128 + h:B_R + R_AG // 128 + h + 1], scale=1.0),
                 (gbank, "csb"), (f"ags{h}", "z0"))

        def conv_stages():
            acc = [falloc() for _ in range(4)]
            sqa, sqb, rstd_t = falloc(), falloc(), falloc()
            mean_t = sqb

            def c0():
                for j in range(4):
                    wcol = lambda k, j=j: csb[:, C_CW + j * CONVK + k:C_CW + j * CONVK + k + 1]
                    T.op("dve", TS(ft[acc[j]][:, :], uin[:, j, 1:513], wcol(0), csb[:, C_CV + j:C_CV + j + 1],
                                   ALU.mult, ALU.add), (f"uin{j}", "csb"), (f"f{acc[j]}",))
                    for k in range(1, CONVK):
                        T.op("dve", STT(ft[acc[j]][:, :], uin[:, j, 1 + k:513 + k], wcol(k), ft[acc[j]][:, :],
                                        ALU.mult, ALU.add), (f"uin{j}", "csb", f"f{acc[j]}"), (f"f{acc[j]}",))

            def sq_pair(j0):
                for j, s in ((j0, sqa), (j0 + 1, sqb)):
                    T.op("pool", TT(ft[s][:, :], ft[acc[j]][:, :], ft[acc[j]][:, :], ALU.mult), (f"f{acc[j]}",),
                         (f"f{s}",))

            def stat_mm(j0):
                T.group("pe", [MM(banks["AUX"], zrow0_f, ft[acc[j0]][:, :], start=(j0 == 0), stop=False),
                               MM(banks["AUX"], zrow64_f, ft[sqa][:, :], start=False, stop=False),
                               MM(banks["AUX"], zrow0_f, ft[acc[j0 + 1]][:, :], start=False, stop=False),
                               MM(banks["AUX"], zrow64_f, ft[sqb][:, :], start=False, stop=(j0 == 2))],
                        ("csb", f"f{acc[j0]}", f"f{acc[j0 + 1]}", f"f{sqa}", f"f{sqb}"), ("AUX",))

            def c1():
                sq_pair(0)

            def c2():
                stat_mm(0)
                sq_pair(2)

            def c3():
                stat_mm(2)
                T.op("dve", CP(ft[sqa][:, :], banks["AUX"]), ("AUX",), (f"f{sqa}",))

            def c4():
                T.group("pe", [MM(banks["AUX"], indA_f, ft[sqa][:, :])], ("csb", f"f{sqa}"), ("AUX",))
                T.op("dve", TS(ft[mean_t][:, :], banks["AUX"], 1.0 / 512.0, None, ALU.mult), ("AUX",),
                     (f"f{mean_t}",))

            def c5():
                T.group("pe", [MM(banks["AUX"], indB_f, ft[sqa][:, :])], ("csb", f"f{sqa}"), ("AUX",))
                T.op("pool", TT(ft[sqa][:, :], ft[mean_t][:, :], ft[mean_t][:, :], ALU.mult), (f"f{mean_t}",),
                     (f"f{sqa}",))
                T.op("dve", STT(ft[rstd_t][:, :], banks["AUX"], 1.0 / 512.0, ft[sqa][:, :], ALU.mult, ALU.subtract),
                     ("AUX", f"f{sqa}"), (f"f{rstd_t}",))

            def c6():
                T.op("act", ACTF(ft[rstd_t][:, :], ft[rstd_t][:, :], AF.Ln, bias=dyn[:, Y_EPS:Y_EPS + 1], scale=1.0),
                     (f"f{rstd_t}", "dyn"), (f"f{rstd_t}",))
                T.op("act", ACTF(ft[rstd_t][:, :], ft[rstd_t][:, :], AF.Exp, scale=-0.5), (f"f{rstd_t}",),
                     (f"f{rstd_t}",))
                T.op("dve", STT(ft[mean_t][:, :], ft[mean_t][:, :], -1.0, ft[rstd_t][:, :], ALU.mult, ALU.mult),
                     (f"f{mean_t}", f"f{rstd_t}"), (f"f{mean_t}",))
                for j in range(4):
                    a = acc[j]
                    T.op("pool", TT(ft[a][:, :], ft[a][:, :], ft[rstd_t][:, :], ALU.mult), (f"f{a}", f"f{rstd_t}"),
                         (f"f{a}",))
                    T.op("pool", TT(ft[a][:, :], ft[a][:, :], ft[mean_t][:, :], ALU.add), (f"f{a}", f"f{mean_t}"),
                         (f"f{a}",))

            def c7():
                for j in range(4):
                    a = acc[j]
                    T.op("act", ACTF(ft[a][:, :], ft[a][:, :], AF.Silu, bias=csb[:, C_CV + 8 + j:C_CV + 9 + j],
                                     scale=csb[:, C_CV + 4 + j:C_CV + 5 + j]), (f"f{a}", "csb"), (f"f{a}",))

            def cg_a(j):
                def f():
                    proj_group("AUX", R_CG + j * 128, lambda c: xb[0][:, c, :], "xall")
                return f

            def cg_b(j):
                def f():
                    a = acc[j]
                    bcg = csb[:, B_R + (R_CG // 128) + j:B_R + (R_CG // 128) + j + 1]
                    T.op("act", ACTF(ft[sqa][:, :], banks["AUX"], AF.Exp, bias=dyn[:, 9 + j:10 + j], scale=-1.0),
                         ("AUX", "dyn"), (f"f{sqa}",))
                    T.op("dve", TS(ft[sqb][:, :], banks["AUX"], bcg, None, ALU.add), ("AUX", "csb"), (f"f{sqb}",))
                    T.op("dve", TS(ft[sqa][:, :], ft[sqa][:, :], 1.0, None, ALU.add), (f"f{sqa}",), (f"f{sqa}",))
                    T.op("dve", RECIP(ft[sqa][:, :], ft[sqa][:, :]), (f"f{sqa}",), (f"f{sqa}",))
                    T.op("dve", TT(ft[sqb][:, :], ft[sqb][:, :], ft[sqa][:, :], ALU.mult), (f"f{sqa}", f"f{sqb}"),
                         (f"f{sqb}",))
                    T.op("dve", TT(U2[:, j, :], ft[a][:, :], ft[sqb][:, :], ALU.mult), (f"f{a}", f"f{sqb}"),
                         (f"u2_{j}",))
                    if j == 3:
                        ffree(sqa, sqb, rstd_t, *acc)
                return f

            st = [c1, c2, c3, c4, c5, c6, c7]
            for j in range(4):
                st += [cg_a(j), cg_b(j)]
            holds = {c2}
            for f in st:
                if f.__name__ == "f" and st.index(f) >= 7 and (st.index(f) - 7) % 2 == 0:
                    holds.add(f)
            return c0, [(f, f in holds) for f in st]

        c0, cstages = conv_stages()
        c0()
        cq = list(cstages)
        fq = []
        CONV_MIN_STEP = 100
        aux_live = [False]

        def pop_conv():
            f, holds = cq.pop(0)
            f()
            aux_live[0] = holds

        nsb = S // 512
        nsteps = nsb * 4
        item_base = state.setdefault("item", 0)
        steps = [(h, stp) for h in range(4) for stp in range(nsteps)]

        def ensure_dma(upto):
            while state["next_dma"] <= min(upto, len(items) - 1):
                issue_kv(state["next_dma"])
                state["next_dma"] += 1

        def qk(g):
            h, stp = steps[g]
            n = item_base + h * nsb + stp // 4
            ensure_dma(n + PF)
            slot = n % 8
            kc = stp % 4
            b = g % 2
            T.group("pe", [
                MM(pS[b][:, 0:512], Wkv[0:64, slot, kc * 128:(kc + 1) * 128], QT[0:64, h, :]),
                MM(pS[b][:, 512:1024], Wkv[64:128, slot, kc * 128:(kc + 1) * 128], QT[64:128, h, :]),
            ], (f"kv{slot}", f"qt{h}"), (f"S{b}a", f"S{b}b"))
            return slot, kc

        info = {}
        info[0] = qk(0)
        if len(steps) > 1:
            info[1] = qk(1)
        last_pop = -10
        for g, (h, stp) in enumerate(steps):
            b = g % 2
            pb = g % 4
            slot, kc = info.pop(g)
            T.op("act", ACTF(pT[pb], pS[b][:, 0:1024], AF.Exp, scale=0.125), (f"S{b}a", f"S{b}b"), (pT_key[pb],))
            if g + 2 < len(steps):
                info[g + 2] = qk(g + 2)
            vch = Wkv[:, slot, 512 + kc * 128:512 + (kc + 1) * 128]
            first, last = (stp == 0), (stp == nsteps - 1)
            mms = [MM(banks["O0"], vch, pT[pb][:, 0:512], start=first, stop=last),
                   MM(banks["O1"], vch, pT[pb][:, 512:1024], start=first, stop=last)]
            rk = (f"kv{slot}", pT_key[pb], "cbf")
            wk = ("O0", "O1")
            if stp % 2 == 1:
                pp = (g - 1) % 4
                f2, l2 = (stp == 1), last
                mms += [
                    MM(banks["SUM"][0:1, :], ones1_bf, pT[pp][:, 0:512], start=f2, stop=l2, tile_position=(0, 0)),
                    MM(banks["SUM"][32:33, :], ones1_bf, pT[pb][:, 0:512], start=f2, stop=l2, tile_position=(0, 32)),
                    MM(banks["SUM"][64:65, :], ones1_bf, pT[pp][:, 512:1024], start=f2, stop=l2,
                       tile_position=(0, 64)),
                    MM(banks["SUM"][96:97, :], ones1_bf, pT[pb][:, 512:1024], start=f2, stop=l2,
                       tile_position=(0, 96)),
                ]
                rk = rk + (pT_key[pp],)
                wk = ("O0", "O1", "SUM")
            T.group("pe", mms, rk, wk)
            if g - last_pop >= 6:
                if aux_live[0]:
                    pop_conv()
                    last_pop = g
                elif fq:
                    fq.pop(0)()
                    last_pop = g
                elif cq and g >= CONV_MIN_STEP:
                    pop_conv()
                    last_pop = g
            if last:
                while aux_live[0]:
                    pop_conv()
                while fq:
                    fq.pop(0)()
                s0, fst = finalize_stages(h)
                s0()
                fq = list(fst)
                last_pop = g
        state["item"] = item_base + 4 * nsb
        while aux_live[0]:
            pop_conv()
        while fq:
            fq.pop(0)()
        while cq:
            pop_conv()

        for k in range(8):
            bya, byc, bmc, bma = ("S0a", "S0b", "S1a", "S1b") if k % 2 == 0 else ("O0", "O1", "SUM", "AUX")
            T.group("pe", [MM(banks[bya], Wao[:, hh, k * 128:(k + 1) * 128], xb[1][:, 4 + hh, :],
                              start=(hh == 0), stop=(hh == 3)) for hh in range(4)],
                    ("Wao", "og0", "og1", "og2", "og3"), (bya,))
            T.group("pe", [MM(banks[byc], Wcp[:, j, k * 128:(k + 1) * 128], U2[:, j, :],
                              start=(j == 0), stop=(j == 3)) for j in range(4)],
                    ("Wcp", "u2_0", "u2_1", "u2_2", "u2_3"), (byc,))
            proj_group(bmc, R_MC + k * 128, lambda c: xb[0][:, c, :], "xall")
            proj_group(bma, R_MA + k * 128, lambda c: xb[0][:, c, :], "xall")
            s1, s2 = falloc(), falloc()
            T.op("act", ACTF(ft[s1][:, :], banks[bmc], AF.Sigmoid,
                             bias=csb[:, B_R + R_MC // 128 + k:B_R + R_MC // 128 + k + 1], scale=1.0),
                 (bmc, "csb"), (f"f{s1}",))
            T.op("act", ACTF(ft[s2][:, :], banks[bma], AF.Sigmoid,
                             bias=csb[:, B_R + R_MA // 128 + k:B_R + R_MA // 128 + k + 1], scale=1.0),
                 (bma, "csb"), (f"f{s2}",))
            T.op("dve", STT(ft[s1][:, :], banks[byc], csb[:, B_CP + k:B_CP + k + 1], ft[s1][:, :], ALU.add, ALU.mult),
                 (byc, "csb", f"f{s1}"), (f"f{s1}",))
            T.op("dve", TT(ft[s2][:, :], banks[bya], ft[s2][:, :], ALU.mult), (bya, f"f{s2}"), (f"f{s2}",))
            T.op("dve", TT(merged[k], ft[s1][:, :], ft[s2][:, :], ALU.add), (f"f{s1}", f"f{s2}"),
                 (merged_key[k], f"mg{k}"))
            ffree(s1, s2)
        if qbi + 1 < len(qb_list):
            emit_xload(qbi + 1, qb_list[qbi + 1][3])
            emit_glu()
        mkeys = tuple(f"mg{k}" for k in range(8))
        for j in range(4):
            zi = (qbi * 4 + j) % 2
            z = zbuf[zi]
            zk = f"z{zi}"
            xt0, xt1 = falloc(), falloc()
            T.dma("sp", f"df{xt0}", [DMA(ft[xt0][:, :], x_own[o0 + j * 128:o0 + (j + 1) * 128, 0:512])],
                  writes=(f"f{xt0}",))
            T.dma("sp", f"df{xt1}", [DMA(ft[xt1][:, :], x_own[o0 + j * 128:o0 + (j + 1) * 128, 512:1024])],
                  writes=(f"f{xt1}",))
            for n, xt in ((0, xt0), (1, xt1)):
                bk = (("S0a", "S0b"), ("S1a", "S1b"))[j % 2][n]
                T.group("pe", [MM(banks[bk], merged[k][:, j * 128:(j + 1) * 128], Wo[:, k, n * 512:(n + 1) * 512],
                                  start=(k == 0), stop=(k == 7)) for k in range(8)], ("Wo",) + mkeys + tuple(pT_key), (bk,))
                T.op("dve", TT(z[:, n * 512:(n + 1) * 512], banks[bk], bro[0][:, n * 512:(n + 1) * 512], ALU.add),
                     (bk, "bro"), (zk,) + zalias[zi])
                T.op("dve", STT(z[:, n * 512:(n + 1) * 512], ft[xt][:, :], ALPHA, z[:, n * 512:(n + 1) * 512],
                                 ALU.mult, ALU.add), (f"f{xt}", zk), (zk,))
            T.op("act", ACTF(z, z, AF.Identity, accum_out=small[:, 0:1]), (zk,), (zk, "small"))
            for n, xt in ((0, xt0), (1, xt1)):
                T.op("act", ACTF(ft[xt][:, :], z[:, n * 512:(n + 1) * 512], AF.Square, accum_out=small[:, 2 + n:3 + n]),
                     (zk,), (f"f{xt}", "small"))
            T.op("dve", TS(small[:, 1:2], small[:, 0:1], 1.0 / D, None, ALU.mult), ("small",), ("small",))
            T.op("dve", TT(small[:, 4:5], small[:, 2:3], small[:, 3:4], ALU.add), ("small",), ("small",))
            T.op("dve", TT(small[:, 6:7], small[:, 1:2], small[:, 1:2], ALU.mult), ("small",), ("small",))
            T.op("dve", STT(small[:, 4:5], small[:, 4:5], 1.0 / D, small[:, 6:7], ALU.mult, ALU.subtract),
                 ("small",), ("small",))
            T.op("act", ACTF(small[:, 4:5], small[:, 4:5], AF.Ln, bias=dyn[:, Y_EPS:Y_EPS + 1], scale=1.0),
                 ("small", "dyn"), ("small",))
            T.op("act", ACTF(small[:, 5:6], small[:, 4:5], AF.Exp, scale=-0.5), ("small",), ("small",))
            T.op("dve", STT(small[:, 7:8], small[:, 1:2], -1.0, small[:, 5:6], ALU.mult, ALU.mult), ("small",),
                 ("small",))
            T.op("act", ACTF(z, z, AF.Identity, bias=small[:, 7:8], scale=small[:, 5:6]), (zk, "small"), (zk,))
            T.op("dve", TT(z, z, bro[1][:, :], ALU.mult), (zk, "bro"), (zk,))
            T.op("pool", TT(z, z, bro[2][:, :], ALU.add), (zk, "bro"), (zk,))
            T.dma("pool", f"dout{zi}", [DMA(y_own[o0 + j * 128:o0 + (j + 1) * 128, :], z)], reads=(zk,),
                  writes=("yout",))
            ffree(xt0, xt1)

    T.barrier()
    with nc.Block() as block:
        T.replay(block)
    es.close()
    return nc


FULL_SEGS = [(16384, 2048), (8192, 1024), (8192, 1024)]


def rope_tables(npos):
    pos = np.arange(npos, dtype=np.float64)
    inv_freq = 1.0 / np.power(10000.0, np.arange(0, HD, 2, dtype=np.float64) / HD)
    ang = pos[:, None] * inv_freq[None, :]
    cos = np.cos(ang).astype(np.float32)
    sin = np.sin(ang).astype(np.float32)
    cos64 = np.concatenate([cos, cos], axis=1)
    sin64 = np.concatenate([-sin, sin], axis=1)
    cosT = np.ascontiguousarray(np.concatenate([cos64, cos64], axis=1).T)
    sinT = np.ascontiguousarray(np.concatenate([sin64, sin64], axis=1).T)
    return cosT, sinT


def prep_inputs(segs, xs, w):
    ncore = NCORES
    x_all = np.concatenate(xs, axis=0)
    NTOK = x_all.shape[0]
    xT_all = np.ascontiguousarray(x_all.T)
    SMAX = max(s for s, _ in segs)
    cosk, sink = rope_tables(SMAX)
    w_in = w["w_in"]
    b_in = w["b_in"]
    w_kv = np.ascontiguousarray(w_in[:, 2048:3072])
    w_r = np.ascontiguousarray(np.concatenate([w_in[:, :2048], w_in[:, 3072:]], axis=1))
    b_rest = np.concatenate([b_in[:2048], b_in[3072:]])
    b_r = np.ascontiguousarray(b_rest.reshape(36, 128).T)
    b_k = np.ascontiguousarray(b_in[2048:2560].reshape(4, 128).T)
    b_v = np.ascontiguousarray(b_in[2560:3072].reshape(4, 128).T)
    perm = np.zeros((128, 128), np.float32)
    for m in range(128):
        d = m % 64
        perm[m + 32 if d < 32 else m - 32, m] = 1.0
    convw = np.ascontiguousarray(w["conv_dw"].T.reshape(4, 128, CONVK).transpose(1, 0, 2).reshape(128, 4 * CONVK))
    cvec = np.concatenate([w["conv_dw_b"].reshape(4, 128).T, w["conv_ln_g"].reshape(4, 128).T,
                           w["conv_ln_b"].reshape(4, 128).T], axis=1)
    cv = np.zeros((128, 1024), np.float32)
    cv[:, 0:36] = b_r
    cv[:, 36:40] = b_k
    cv[:, 40:44] = b_v
    cv[:, 44:52] = w["b_conv_proj"].reshape(8, 128).T
    cv[:, 52] = w["subln_g"]
    cv[:, 56:60] = np.array([1.0, 0.0, 0.0, 1.0], np.float32)[None, :]
    cv[:, 64:188] = convw
    cv[:, 188:200] = cvec
    cv[:, 200:328] = 1.0
    cv[:, 328:456] = perm
    cv[0, 456:584] = 1.0
    cv[32, 456:584] = 1.0
    cv[64, 584:712] = 1.0
    cv[96, 584:712] = 1.0
    cv[:, 712 + 64] = 1.0
    lamv = np.concatenate([w["lam_q1"], w["lam_k1"], w["lam_q2"], w["lam_k2"]]).reshape(1, 256).astype(np.float32)
    rowv = np.ascontiguousarray(np.stack([w["b_out"], w["ln_g"], w["ln_b"]], axis=0))
    shared = dict(xT_all=xT_all, w_kv=w_kv, w_r=w_r, cvals=cv, cosk=cosk, sink=sink,
                  w_cp=np.ascontiguousarray(w["w_conv_proj"]), lamv=lamv,
                  w_ao=np.ascontiguousarray(w["w_attn_o"]), w_o=np.ascontiguousarray(w["w_out"]), rowv=rowv)
    maps = []
    for c in range(ncore):
        xo_cols, xown_rows, cq, sq, hms = [], [], [], [], []
        for (S, nq), x in zip(segs, xs):
            a = c * nq
            blk = np.zeros((nq + 2 * HALO, D), np.float32)
            lo, hi = a - HALO, a + nq + HALO
            slo, shi = max(lo, 0), min(hi, S)
            blk[slo - lo:shi - lo] = x[slo:shi]
            xo_cols.append(blk.T)
            xown_rows.append(x[a:a + nq])
            cq.append(cosk[:, a:a + nq])
            sq.append(sink[:, a:a + nq])
            for qb in range(nq // 512):
                q0 = a + qb * 512
                idx = np.concatenate([np.arange(q0 - HALO, q0), np.arange(q0 + 512, q0 + 512 + HALO)])
                valid = ((idx >= 0) & (idx < S)).astype(np.float32)
                hms.append(np.broadcast_to(valid[None, :], (128, 32)))
        m = dict(shared)
        m["xT_own"] = np.ascontiguousarray(np.concatenate(xo_cols, axis=1))
        m["x_own"] = np.ascontiguousarray(np.concatenate(xown_rows, axis=0))
        m["cosq"] = np.ascontiguousarray(np.concatenate(cq, axis=1))
        m["sinq"] = np.ascontiguousarray(np.concatenate(sq, axis=1))
        m["hmask"] = np.ascontiguousarray(np.concatenate(hms, axis=1))
        maps.append(m)
    return maps


_PROG_CACHE = {}


def run_layer(segs, xs, w, trace=False):
    maps = prep_inputs(segs, xs, w)
    key = tuple(segs)
    if key not in _PROG_CACHE:
        _PROG_CACHE[key] = build_program(segs)
    nc = _PROG_CACHE[key]
    res = run_bass_kernel_spmd(nc, maps, core_ids=list(range(NCORES)), **({"trace": True} if trace else {}))
    outs = []
    off = 0
    for S, nq in segs:
        y = np.empty((S, D), np.float32)
        for c in range(NCORES):
            y[c * nq:(c + 1) * nq] = res.results[c]["y_own"][off:off + nq]
        outs.append(y)
        off += nq
    return outs, res


def kernel(x_prompt, x_sample, w_in, b_in, conv_dw, conv_dw_b, conv_ln_g, conv_ln_b, w_conv_proj, b_conv_proj,
           lam_q1, lam_k1, lam_q2, lam_k2, subln_g, w_attn_o, w_out, b_out, ln_g, ln_b):
    f = lambda a: np.ascontiguousarray(np.asarray(a, dtype=np.float32))
    w = dict(w_in=f(w_in)[0], b_in=f(b_in)[0], conv_dw=f(conv_dw)[0], conv_dw_b=f(conv_dw_b)[0],
             conv_ln_g=f(conv_ln_g)[0], conv_ln_b=f(conv_ln_b)[0], w_conv_proj=f(w_conv_proj)[0],
             b_conv_proj=f(b_conv_proj)[0], lam_q1=f(lam_q1)[0], lam_k1=f(lam_k1)[0], lam_q2=f(lam_q2)[0],
             lam_k2=f(lam_k2)[0], subln_g=f(subln_g)[0], w_attn_o=f(w_attn_o)[0], w_out=f(w_out)[0],
             b_out=f(b_out)[0], ln_g=f(ln_g)[0], ln_b=f(ln_b)[0])
    xp = f(x_prompt)
    xsm = f(x_sample)
    xs = [xp[0], xsm[0], xsm[1]]
    outs, _ = run_layer(FULL_SEGS, xs, w)
    y_prompt = outs[0][None]
    y_sample = np.stack([outs[1], outs[2]], axis=0)
    return (y_prompt, y_sample)
```
